# Optimizing a Trainium2 kernel written in Bass

```python
import jax, jax.numpy as jnp
from jax import lax
import numpy as np

D_MODEL = 1024
BATCH = 16
SEQ = 2048
DEPTH = 1

MEM_LEN = 256
HEAD_DIM = 64
CHUNK = 128
A_GROUPS = 4
A_WIDTH = D_MODEL // 2
A_GROUP_W = A_WIDTH // A_GROUPS
SWA_HEADS = 4
SWA_KV_HEADS = 2
SWA_WIDTH = SWA_HEADS * HEAD_DIM
SWA_KV_WIDTH = SWA_KV_HEADS * HEAD_DIM
WINDOW = 128
MEM_HEADS = 4
MEM_WIDTH = MEM_HEADS * HEAD_DIM
MIX_WIDTH = A_WIDTH + SWA_WIDTH + MEM_WIDTH
IN_WIDTH = 2 * A_WIDTH + SWA_WIDTH + 2 * SWA_KV_WIDTH + MEM_WIDTH + MIX_WIDTH
N_BUCKETS = 32
MAX_DISTANCE = 128
EPS = 1e-6
NEG = -1e30

kernel_name = "hymba_gmlp_swa_sink_memxattn_layer"


def rms_norm(x, g):
    xf = x.astype(jnp.float32)
    y = xf * lax.rsqrt(jnp.mean(xf * xf, axis=-1, keepdims=True) + EPS)
    return (y * g.astype(jnp.float32)).astype(x.dtype)


def t5_causal_buckets(dist):
    n = np.maximum(dist, 0)
    max_exact = N_BUCKETS // 2
    large = max_exact + (np.log(np.maximum(n, 1) / max_exact) / np.log(MAX_DISTANCE / max_exact)
                         * (N_BUCKETS - max_exact)).astype(np.int32)
    large = np.minimum(large, N_BUCKETS - 1)
    return np.where(n < max_exact, n, large).astype(np.int32)


def chunked_spatial_gating(u, v, v_g, v_b, w_s, b_s):
    b, s, _ = u.shape
    nc = s // CHUNK
    vg = v.reshape(b, s, A_GROUPS, A_GROUP_W).astype(jnp.float32)
    mu = jnp.mean(vg, axis=-1, keepdims=True)
    var = jnp.mean(jnp.square(vg - mu), axis=-1, keepdims=True)
    vg = (vg - mu) * lax.rsqrt(var + EPS)
    vg = vg * v_g.reshape(A_GROUPS, A_GROUP_W).astype(jnp.float32) + v_b.reshape(A_GROUPS, A_GROUP_W).astype(jnp.float32)
    vc = vg.astype(v.dtype).reshape(b, nc, CHUNK, A_GROUPS, A_GROUP_W)
    causal = jnp.tril(jnp.ones((CHUNK, CHUNK), dtype=w_s.dtype))
    w = w_s * causal[None]
    sv = jnp.einsum('gts,bnsgc->bntgc', w, vc) + b_s.T[None, None, :, :, None]
    return u * sv.reshape(b, s, A_WIDTH)


def sliding_window_attention(q, k, v, sinks, rel_bias):
    b, s, hq, dh = q.shape
    nb = s // CHUNK
    g = hq // SWA_KV_HEADS
    qb = q.reshape(b, nb, CHUNK, SWA_KV_HEADS, g, dh)

    def band(t):
        tb = t.reshape(b, nb, CHUNK, SWA_KV_HEADS, dh)
        prev = jnp.pad(tb, ((0, 0), (1, 0), (0, 0), (0, 0), (0, 0)))[:, :-1]
        return jnp.concatenate([prev, tb], axis=2)

    kb, vb = band(k), band(v)
    logits = jnp.einsum('bnqhgd,bnjhd->bnhgqj', qb, kb).astype(jnp.float32) * (dh ** -0.5)

    qi = np.arange(CHUNK)[:, None]
    kj = np.arange(2 * CHUNK)[None, :]
    dist = qi + CHUNK - kj
    blk = np.arange(nb)[:, None, None]
    valid = (dist >= 0) & (dist < WINDOW) & (blk * CHUNK + kj - CHUNK >= 0)
    buckets = t5_causal_buckets(dist)
    bias = rel_bias.astype(jnp.float32)[buckets]
    bias = jnp.transpose(bias, (2, 0, 1)).reshape(SWA_KV_HEADS, g, CHUNK, 2 * CHUNK)

    logits = jnp.where(valid[None, :, None, None], logits + bias[None, None], NEG)
    sink = sinks.astype(jnp.float32).reshape(1, 1, SWA_KV_HEADS, g, 1, 1)
    m = jnp.maximum(jnp.max(logits, axis=-1, keepdims=True), sink)
    p = jnp.exp(logits - m)
    probs = p / (jnp.sum(p, axis=-1, keepdims=True) + jnp.exp(sink - m))
    out = jnp.einsum('bnhgqj,bnjhd->bnqhgd', probs.astype(v.dtype), vb)
    return out.reshape(b, s, hq * dh)


def memory_cross_attention(q, mem_k, mem_v):
    b, s, h, dh = q.shape
    logits = jnp.einsum('bshd,bmhd->bhsm', q, mem_k).astype(jnp.float32) * (dh ** -0.5)
    probs = jax.nn.softmax(logits, axis=-1)
    out = jnp.einsum('bhsm,bmhd->bshd', probs.astype(mem_v.dtype), mem_v)
    return out.reshape(b, s, h * dh)


def setup_inputs(seed: int = 0) -> dict:
    key = jax.random.key(seed)
    ks = jax.random.split(key, 16)
    f32 = jnp.float32
    x = jax.random.normal(ks[0], (BATCH, SEQ, D_MODEL), f32)
    mem = jax.random.normal(ks[1], (BATCH, MEM_LEN, D_MODEL), f32)
    pre_norm_g = 1.0 + 0.05 * jax.random.normal(ks[2], (DEPTH, D_MODEL), f32)
    post_norm_g = 1.0 + 0.05 * jax.random.normal(ks[3], (DEPTH, D_MODEL), f32)
    mem_norm_g = 1.0 + 0.05 * jax.random.normal(ks[4], (DEPTH, D_MODEL), f32)
    w_in = jax.random.normal(ks[5], (DEPTH, D_MODEL, IN_WIDTH), f32) * D_MODEL ** -0.5
    w_mem_kv = jax.random.normal(ks[6], (DEPTH, D_MODEL, 2 * MEM_WIDTH), f32) * D_MODEL ** -0.5
    v_norm_g = 1.0 + 0.05 * jax.random.normal(ks[7], (DEPTH, A_WIDTH), f32)
    v_norm_b = 0.02 * jax.random.normal(ks[8], (DEPTH, A_WIDTH), f32)
    w_spatial = jax.random.normal(ks[9], (DEPTH, A_GROUPS, CHUNK, CHUNK), f32) * CHUNK ** -0.5
    b_spatial = 1.0 + 0.1 * jax.random.normal(ks[10], (DEPTH, A_GROUPS, CHUNK), f32)
    attn_sinks = 0.5 * jax.random.normal(ks[11], (DEPTH, SWA_HEADS), f32)
    rel_bias = 0.5 * jax.random.normal(ks[12], (N_BUCKETS, SWA_HEADS), f32)
    w_out = jax.random.normal(ks[13], (DEPTH, MIX_WIDTH, D_MODEL), f32) * MIX_WIDTH ** -0.5
    return {"x": x, "mem": mem, "pre_norm_g": pre_norm_g, "post_norm_g": post_norm_g,
            "mem_norm_g": mem_norm_g, "w_in": w_in, "w_mem_kv": w_mem_kv,
            "v_norm_g": v_norm_g, "v_norm_b": v_norm_b, "w_spatial": w_spatial,
            "b_spatial": b_spatial, "attn_sinks": attn_sinks, "rel_bias": rel_bias,
            "w_out": w_out}


def reference(x, mem, pre_norm_g, post_norm_g, mem_norm_g, w_in, w_mem_kv, v_norm_g, v_norm_b,
              w_spatial, b_spatial, attn_sinks, rel_bias, w_out):
    b, s, _ = x.shape
    m_len = mem.shape[1]
    split_at = np.cumsum([A_WIDTH, A_WIDTH, SWA_WIDTH, SWA_KV_WIDTH, SWA_KV_WIDTH, MEM_WIDTH]).tolist()
    for layer in range(DEPTH):
        h = rms_norm(x, pre_norm_g[layer])
        proj = h @ w_in[layer]
        a_u, a_v, sq, sk, sv, mq, z = jnp.split(proj, split_at, axis=-1)

        y_a = chunked_spatial_gating(jax.nn.gelu(a_u), jax.nn.gelu(a_v), v_norm_g[layer],
                                     v_norm_b[layer], w_spatial[layer], b_spatial[layer])

        y_b = sliding_window_attention(sq.reshape(b, s, SWA_HEADS, HEAD_DIM),
                                       sk.reshape(b, s, SWA_KV_HEADS, HEAD_DIM),
                                       sv.reshape(b, s, SWA_KV_HEADS, HEAD_DIM),
                                       attn_sinks[layer], rel_bias)

        mkv = rms_norm(mem, mem_norm_g[layer]) @ w_mem_kv[layer]
        mk, mv = jnp.split(mkv, 2, axis=-1)
        y_c = memory_cross_attention(mq.reshape(b, s, MEM_HEADS, HEAD_DIM),
                                     mk.reshape(b, m_len, MEM_HEADS, HEAD_DIM),
                                     mv.reshape(b, m_len, MEM_HEADS, HEAD_DIM))

        y = jnp.concatenate([y_a, y_b, y_c], axis=-1) * jax.nn.silu(z)
        x = x + rms_norm(y @ w_out[layer], post_norm_g[layer])
    return x
```

```python
import numpy as np
import concourse.bass as bass
import concourse.mybir as mybir
from concourse.bass_utils import run_bass_kernel_spmd

F32 = mybir.dt.float32
BF16 = mybir.dt.bfloat16
AF = mybir.ActivationFunctionType
ALU = mybir.AluOpType

P = 128
NT = 32
D = 1024
INW = 2816
EPS = 1e-6
NEG = -30000.0
VW = 68
C_U, C_Q, C_K, C_MQ, C_Z, C_V, C_SV = 0, 512, 768, 896, 1152, 2176, 2688


class Buf:
    __slots__ = ("name", "w", "r")

    def __init__(self, name):
        self.name = name
        self.w = None
        self.r = []


class Ins:
    __slots__ = ("eng", "fn", "deps", "key", "val", "need_inc", "is_dma", "final")


class Prog:
    def __init__(self):
        self.ins = []
        self.store_ids = []

    def op(self, eng, fn, reads=(), writes=(), excl=(), dma_key=None, final=False, is_store=False):
        i = len(self.ins)
        cand = set()
        for b in reads:
            if b.w is not None:
                cand.add(b.w)
        for b in writes:
            if b.w is not None:
                cand.add(b.w)
            cand.update(b.r)
        deps = set()
        for d in cand:
            di = self.ins[d]
            if di.is_dma or dma_key is not None or di.eng != eng or eng != "pe":
                deps.add(d)
        for b in excl:
            if b.w is not None and self.ins[b.w].eng != eng:
                deps.add(b.w)
        for b in reads:
            b.r.append(i)
        for b in writes:
            b.w = i
            b.r = []
        for b in excl:
            b.w = i
        x = Ins()
        x.eng = eng
        x.fn = fn
        x.deps = deps
        x.is_dma = dma_key is not None
        x.key = dma_key if dma_key is not None else eng
        x.val = None
        x.need_inc = x.is_dma
        x.final = final
        self.ins.append(x)
        if is_store:
            self.store_ids.append(i)
        return i

    def finish(self):
        x = Ins()
        x.eng = "sp"
        x.fn = None
        x.deps = set(self.store_ids)
        x.is_dma = False
        x.key = "sp"
        x.val = None
        x.need_inc = False
        x.final = False
        self.ins.append(x)

    def emit(self, nc):
        ins = self.ins
        for x in ins:
            for d in x.deps:
                ins[d].need_inc = True
        counts = {}
        for x in ins:
            if x.need_inc:
                step = 16 if x.is_dma else 1
                counts[x.key] = counts.get(x.key, 0) + step
                x.val = counts[x.key]
        sems = {k: nc.alloc_semaphore("s_" + str(k)) for k in counts}
        by_eng = {e: [] for e in ("pe", "act", "dve", "pool", "sp")}
        for x in ins:
            by_eng[x.eng].append(x)

        def run(eng_name, e):
            waited = {}
            for x in by_eng[eng_name]:
                need = {}
                for d in x.deps:
                    di = ins[d]
                    v = counts[di.key] if di.final else di.val
                    if v > need.get(di.key, 0):
                        need[di.key] = v
                for k, v in need.items():
                    if waited.get(k, 0) < v:
                        e.wait_ge(sems[k], v)
                        waited[k] = v
                if x.fn is None:
                    continue
                r = x.fn(e)
                if x.need_inc:
                    r.then_inc(sems[x.key], 16 if x.is_dma else 1)

        with nc.Block() as block:
            @block.tensor
            def _(e):
                run("pe", e)

            @block.scalar
            def _(e):
                run("act", e)

            @block.vector
            def _(e):
                run("dve", e)

            @block.gpsimd
            def _(e):
                run("pool", e)

            @block.sync
            def _(e):
                run("sp", e)


class Ring:
    def __init__(self, nc, name, shape, dt, n):
        self.t = [nc.alloc_sbuf_tensor(f"sb_{name}{i}", list(shape), dt) for i in range(n)]
        self.b = [Buf(f"{name}{i}") for i in range(n)]
        self.n = n


def build_program():
    nc = bass.Bass("TRN2", target_bir_lowering=False)
    pg = Prog()
    op = pg.op

    def din(name, shape):
        return nc.dram_tensor(name, list(shape), F32, kind="ExternalInput").ap()

    x_d = din("x", [NT * P, D])
    mem_d = din("mem", [512, D])
    win_d = din("win", [D, INW])
    wout_d = din("wout", [D, D])
    wmem_d = din("wmem", [D, 512])
    gpre_d = din("gpre", [P, 8])
    gmem_d = din("gmem", [P, 8])
    gpost_d = din("gpost", [P, D])
    vgcol_d = din("vgcol", [P, 4])
    vbb_d = din("vbb", [P, 512])
    wsT_d = din("wsT", [P, 512])
    cmask_d = din("cmask", [P, 512])
    bsrow_d = din("bsrow", [1, 512])
    ebias_d = din("ebias", [P, 1024])
    emask_d = din("emask", [P, 1024])
    sinks_d = din("sinks", [P, 4])
    ident_d = din("ident", [P, P])
    out_d = nc.dram_tensor("out", [NT * P, D], F32, kind="ExternalOutput").ap()

    def S(name, shape, dt=F32):
        return nc.alloc_sbuf_tensor("sb_" + name, list(shape), dt), Buf(name)

    FB = Ring(nc, "F", [P, D], F32, 6)
    XA = (0, 1, 2, 5)
    XD = (3, 4)
    TO = 5
    win_bf = nc.alloc_sbuf_tensor("sb_win_bf", [P, 8, INW], BF16)
    wout_bf = nc.alloc_sbuf_tensor("sb_wout_bf", [P, 8, D], BF16)
    wmem_bf = nc.alloc_sbuf_tensor("sb_wmem_bf", [P, 8, 512], BF16)
    PIECES = [(512, 1152), (2688, 2816), (0, 512), (2176, 2688), (1152, 1664), (1664, 2176)]
    win_b = [[Buf(f"win{k}_{j}") for j in range(len(PIECES))] for k in range(8)]

    def win_bufs(col):
        for j, (c0, c1) in enumerate(PIECES):
            if c0 <= col < c1:
                return [win_b[k][j] for k in range(8)]
        raise ValueError(col)
    wout_b = [Buf(f"wout{h}") for h in range(2)]
    wmem_b = [Buf(f"wmem{k}") for k in range(8)]

    ident_bf, ident_bf_b = S("ident_bf", [P, P], BF16)
    gpre, gpre_b = S("gpre", [P, 8])
    gmem, gmem_b = S("gmem", [P, 8])
    gpost, gpost_b = S("gpost", [P, D])
    vgcol, vgcol_b = S("vgcol", [P, 4])
    wsT_bf, wsT_bf_b = S("wsT_bf", [P, 512], BF16)
    biasA, biasA_b = S("biasA", [P, 512])
    EBT, EBT_b = S("EBT", [P, 1024])
    esink, esink_b = S("esink", [P, 4])
    mhalf, mhalf_b = S("mhalf", [P, 4])
    junk, junk_b = S("junk", [P, 512], BF16)

    ssA = Ring(nc, "ssA", [P, 1], F32, 2)
    rrA = Ring(nc, "rrA", [P, 1], F32, 2)
    xs = Ring(nc, "xs", [P, D], BF16, 2)
    hT = Ring(nc, "hT", [P, 8, 512], BF16, 2)
    hT_b = [[Buf(f"hT{i}_{t}") for t in range(4)] for i in range(2)]
    gu = Ring(nc, "gu", [P, 4, 512], BF16, 2)
    gu_b = [[Buf(f"gu{i}_{c}") for c in range(4)] for i in range(2)]
    sz47 = Ring(nc, "sz47", [P, 4, 512], BF16, 2)
    sz47_b = [[Buf(f"sz47{i}_{c}") for c in range(4)] for i in range(2)]
    sztmp = Ring(nc, "sztmp", [P, 512], BF16, 2)
    qT = Ring(nc, "qT", [P, 2, 512], BF16, 2)
    qT_b = [[Buf(f"qT{i}_{c}") for c in range(2)] for i in range(2)]
    kT = Ring(nc, "kT", [P, 512], BF16, 3)
    mqT = Ring(nc, "mqT", [P, 2, 512], BF16, 2)
    mqT_b = [[Buf(f"mqT{i}_{c}") for c in range(2)] for i in range(2)]
    Vaug = Ring(nc, "Vaug", [P, 4, 2, VW], BF16, 3)
    gv = Ring(nc, "gv", [P, 512], F32, 2)
    st = Ring(nc, "st", [P, 4, 6], F32, 2)
    mv = Ring(nc, "mv", [P, 4, 2], F32, 2)
    rs4 = Ring(nc, "rs4", [P, 4], F32, 2)
    nm4 = Ring(nc, "nm4", [P, 4], F32, 2)
    vhat = Ring(nc, "vhat", [P, 512], BF16, 8)
    tA = Ring(nc, "tA", [P, 512], F32, 2)
    st_g = [[Buf(f"st{i}_{g}") for g in range(4)] for i in range(2)]
    mv_g = [[Buf(f"mv{i}_{g}") for g in range(4)] for i in range(2)]
    vhat_g = [[Buf(f"vhat{i}_{g}") for g in range(4)] for i in range(8)]
    tA_g = [[Buf(f"tA{i}_{g}") for g in range(4)] for i in range(2)]
    E0 = Ring(nc, "E0", [P, 512], F32, 2)
    E = Ring(nc, "E", [P, 512], BF16, 8)
    Em, _ = S("Em", [P, 4, 2, 512], BF16)
    Em_b = [[Buf(f"Em{h}_{m}") for m in range(2)] for h in range(4)]
    den = Ring(nc, "den", [P, 4], F32, 2)
    rden = Ring(nc, "rden", [P, 4], F32, 2)
    rdm = Ring(nc, "rdm", [P, 4], F32, 2)
    ybc = Ring(nc, "ybc", [P, 512], BF16, 2)
    ybc_s = [Buf("ybcs0"), Buf("ybcs1")]
    ybc_m = [Buf("ybcm0"), Buf("ybcm1")]
    yT = Ring(nc, "yT", [P, 8, P], BF16, 3)
    yT_a = [Buf(f"yTa{i}") for i in range(3)]
    yT_b = [Buf(f"yTb{i}") for i in range(3)]
    ss2 = Ring(nc, "ss2", [P, 2], F32, 2)
    ss2_b = [[Buf(f"ss2{i}_{h}") for h in range(2)] for i in range(2)]
    r2 = Ring(nc, "r2", [P, 1], F32, 2)
    tO_b = [Buf("tO_h0"), Buf("tO_h1")]
    mkT = Ring(nc, "mkT", [P, 2, 256], BF16, 2)
    mkT_b = [[Buf(f"mkT{i}_{c}") for c in range(2)] for i in range(2)]
    mvaug = Ring(nc, "mvaug", [P, 2, 4, VW], BF16, 2)
    mvaug_b = [[Buf(f"mvaug{i}_{c}") for c in range(2)] for i in range(2)]

    banks = [nc.alloc_psum_tensor(f"bk{i}", [P, 512], F32) for i in range(8)]
    bank_b = [Buf(f"bk{i}") for i in range(8)]
    bank_ctr = [0]

    def nb():
        i = bank_ctr[0] % 8
        bank_ctr[0] += 1
        return banks[i], bank_b[i]

    def load_const(dst, src, b):
        op("sp", lambda q: q.dma_start(out=dst, in_=src), writes=[b], dma_key="c_" + b.name)

    ident_f = FB.t[3][:, 640:768]
    bsrow = FB.t[3]
    ones_row_t, ones_row_b = S("ones_row", [1, P])
    ones_row = ones_row_t[0:1, :]
    vbb_t = FB.t[4][:, 0:512]
    wsm = FB.t[4][:, 512:1024]
    wsm_b = FB.b[4]

    def setup_first():
        op("pool", lambda q: q.memset(mhalf[:], -0.5), writes=[mhalf_b])
        op("pool", lambda q: q.memset(ones_row, 1.0), writes=[ones_row_b])
        for i in range(3):
            op("pool", lambda q, i=i: q.memset(Vaug.t[i][:], 1.0), writes=[Vaug.b[i]])
        for i in range(2):
            op("pool", lambda q, i=i: q.memset(mvaug.t[i][:], 1.0), writes=mvaug_b[i])
        op("sp", lambda q: q.dma_start(out=FB.t[3][:, 640:768], in_=ident_d), writes=[FB.b[3]], dma_key="F3")
        load_const(gpre[:], gpre_d, gpre_b)
        load_const(gmem[:], gmem_d, gmem_b)
        op("dve", lambda q: q.tensor_copy(out=ident_bf[:], in_=ident_f), reads=[FB.b[3]], writes=[ident_bf_b])
        op("sp", lambda q: q.dma_start(out=gpost[:, 0:512], in_=wsT_d), writes=[gpost_b], dma_key="c_gpost")
        op("sp", lambda q: q.dma_start(out=gpost[:, 512:1024], in_=cmask_d), writes=[gpost_b], dma_key="c_gpost")
        op("sp", lambda q: q.dma_start(out=vbb_t, in_=vbb_d), writes=[FB.b[4]], dma_key="F4")
        op("sp", lambda q: q.dma_start(out=FB.t[3][0:1, 0:512], in_=bsrow_d), reads=[FB.b[3]], writes=[FB.b[3]],
           dma_key="F3")

    def setup_rest():
        load_const(vgcol[:], vgcol_d, vgcol_b)
        load_const(esink[:], sinks_d, esink_b)
        load_const(EBT[:], ebias_d, EBT_b)
        for hh in range(2):
            op("sp", lambda q, hh=hh: q.dma_start(out=tA.t[hh][:], in_=emask_d[:, hh * 512:(hh + 1) * 512]),
               writes=tA_g[hh], dma_key=f"tA{hh}")

    def setup_compute():
        op("dve", lambda q: q.tensor_tensor(out=wsm, in0=gpost[:, 0:512], in1=gpost[:, 512:1024], op=ALU.mult),
           reads=[gpost_b, FB.b[4]], writes=[wsm_b])
        op("dve", lambda q: q.tensor_copy(out=wsT_bf[:], in_=wsm), reads=[wsm_b], writes=[wsT_bf_b])
        bk, bkb = nb()

        def bias_mm(q):
            r = None
            for g in range(4):
                q.matmul(out=bk[:, g * P:(g + 1) * P], lhsT=vbb_t[:, g * P:(g + 1) * P], rhs=wsm[:, g * P:(g + 1) * P],
                         start=True, stop=False)
                r = q.matmul(out=bk[:, g * P:(g + 1) * P], lhsT=ones_row, rhs=bsrow[0:1, g * P:(g + 1) * P],
                             start=False, stop=True)
            return r
        op("pe", bias_mm, reads=[wsm_b, FB.b[3], ones_row_b], excl=[bkb])
        op("dve", lambda q: q.tensor_copy(out=biasA[:], in_=bk[:]), writes=[biasA_b], excl=[bkb])

    def setup_ebt():
        op("act", lambda q: q.activation(out=EBT[:], in_=EBT[:], func=AF.Exp), reads=[EBT_b], writes=[EBT_b])
        for hh in range(2):
            op("dve", lambda q, hh=hh: q.tensor_tensor(out=EBT[:, hh * 512:(hh + 1) * 512],
                                                       in0=EBT[:, hh * 512:(hh + 1) * 512], in1=tA.t[hh][:], op=ALU.mult),
               reads=[EBT_b] + tA_g[hh], writes=[EBT_b])
        op("act", lambda q: q.activation(out=esink[:], in_=esink[:], func=AF.Exp), reads=[esink_b], writes=[esink_b])

    win_v = win_d.rearrange("(k p) c -> p k c", p=P)
    wout_v = wout_d.rearrange("(k p) c -> p k c", p=P)
    wmem_v = wmem_d.rearrange("(k p) c -> p k c", p=P)

    def load_win_piece(j):
        c0, c1 = PIECES[j]
        op("pool", lambda q: q.dma_start(out=win_bf[:, :, c0:c1], in_=win_v[:, :, c0:c1]),
           writes=[win_b[k][j] for k in range(8)], dma_key=f"W{j}")

    def load_wmem():
        op("pool", lambda q: q.dma_start(out=wmem_bf[:], in_=wmem_v), writes=wmem_b, dma_key="Wm")

    def load_wout():
        for h in range(2):
            op("pool", lambda q, h=h: q.dma_start(out=wout_bf[:, :, h * 512:(h + 1) * 512],
                                                  in_=wout_v[:, :, h * 512:(h + 1) * 512]),
               writes=[wout_b[h]], dma_key=f"Wo{h}")

    a_ctr = [0]

    xa_live = set()

    def A_load(src_ap, force=None):
        if force is not None:
            ft, fb = FB.t[XA[force]], FB.b[XA[force]]
            xa_live.add(force)
            op("sp", lambda q: q.dma_start(out=ft[:], in_=src_ap), writes=[fb], dma_key=f"F{XA[force]}")
            return force
        for d in range(3):
            i = (a_ctr[0] + d) % 3
            if i not in xa_live:
                break
        else:
            raise RuntimeError("no free XA slot")
        a_ctr[0] = i + 1
        xa_live.add(i)
        ft, fb = FB.t[XA[i]], FB.b[XA[i]]
        op("sp", lambda q: q.dma_start(out=ft[:], in_=src_ap), writes=[fb], dma_key=f"F{XA[i]}")
        return i

    ac_ctr = [0]

    def A_pre(i):
        ft, fb = FB.t[XA[i]], FB.b[XA[i]]
        j = ac_ctr[0] % 2
        ac_ctr[0] += 1
        op("act", lambda q: q.activation(out=xs.t[j][:], in_=ft[:], func=AF.Square, accum_out=ssA.t[j][:]),
           reads=[fb], writes=[xs.b[j], ssA.b[j]])
        op("pool", lambda q: q.tensor_scalar(out=rrA.t[j][:], in0=ssA.t[j][:], scalar1=1.0 / D, scalar2=EPS,
                                             op0=ALU.mult, op1=ALU.add), reads=[ssA.b[j]], writes=[rrA.b[j]])
        op("pool", lambda q: q.tensor_tensor(out=rrA.t[j][:], in0=rrA.t[j][:], in1=mhalf[:, 0:1], op=ALU.pow),
           reads=[rrA.b[j], mhalf_b], writes=[rrA.b[j]])
        return j

    def A_pre2(i, j):
        ft, fb = FB.t[XA[i]], FB.b[XA[i]]
        xa_live.discard(i)
        op("dve", lambda q: q.tensor_scalar(out=xs.t[j][:], in0=ft[:], scalar1=rrA.t[j][:, 0:1], scalar2=None,
                                            op0=ALU.mult), reads=[fb, rrA.b[j]], writes=[xs.b[j]])

    def A_tr(j, gam, gam_b, dst_t, dst_col, dst_buf):
        bk, bkb = nb()
        bkv = bk[:].bitcast(BF16).rearrange("p (k c) -> p k c", k=8)

        def tr(q):
            r = None
            for k in range(8):
                r = q.transpose(out=bkv[:, k, :], in_=xs.t[j][:, k * P:(k + 1) * P], identity=ident_bf[:])
            return r
        op("pe", tr, reads=[xs.b[j], ident_bf_b], excl=[bkb])
        op("dve", lambda q: q.tensor_tensor(out=dst_t[:, :, dst_col:dst_col + P], in0=bkv,
                                            in1=gam[:].unsqueeze(2).to_broadcast([P, 8, P]), op=ALU.mult),
           reads=[gam_b], writes=[dst_buf], excl=[bkb])

    def stageA(src_ap, gam, gam_b, dst_t, dst_col, dst_buf):
        i = A_load(src_ap)
        j = A_pre(i)
        A_pre2(i, j)
        A_tr(j, gam, gam_b, dst_t, dst_col, dst_buf)

    mem_xs = {}

    def mem_pre(b):
        for mt in range(2):
            i = A_load(mem_d[(b * 2 + mt) * P:(b * 2 + mt + 1) * P, :])
            j = A_pre(i)
            A_pre2(i, j)
            mem_xs[(b, mt)] = j

    def mem_tr(b):
        for mt in range(2):
            A_tr(mem_xs[(b, mt)], gmem, gmem_b, hT.t[1], mt * P, hT_b[1][mt])

    def mem_kv(b):
        memT = hT.t[1][:, :, 0:256]
        memT_b = hT_b[1][0:2]
        for c in range(2):
            bk, bkb = nb()

            def mm(q, c=c, bk=bk):
                r = None
                for k in range(8):
                    r = q.matmul(out=bk[:, 0:256], lhsT=wmem_bf[:, k, c * P:(c + 1) * P], rhs=memT[:, k, :],
                                 start=(k == 0), stop=(k == 7))
                return r
            op("pe", mm, reads=memT_b + wmem_b, excl=[bkb])
            op("act", lambda q, c=c, bk=bk: q.activation(out=mkT.t[b][:, c, :], in_=bk[:, 0:256], func=AF.Copy),
               writes=[mkT_b[b][c]], excl=[bkb])
        for mb in range(2):
            bk, bkb = nb()

            def mm(q, mb=mb, bk=bk):
                r = None
                for k in range(8):
                    r = q.matmul(out=bk[:, 0:256], lhsT=memT[:, k, mb * P:(mb + 1) * P], rhs=wmem_bf[:, k, 256:512],
                                 start=(k == 0), stop=(k == 7))
                return r
            op("pe", mm, reads=memT_b + wmem_b, excl=[bkb])
            op("dve", lambda q, mb=mb, bk=bk: q.tensor_copy(
                out=mvaug.t[b][:, mb, :, 0:64], in_=bk[:, 0:256].rearrange("p (h d) -> p h d", h=4)),
               writes=[mvaug_b[b][mb]], excl=[bkb])

    def feat_chunk(s, col, evac):
        sb = s % 2
        bk, bkb = nb()

        def mm(q):
            r = None
            for k in range(8):
                r = q.matmul(out=bk[:], lhsT=win_bf[:, k, col:col + P], rhs=hT.t[sb][:, k, :],
                             start=(k == 0), stop=(k == 7))
            return r
        op("pe", mm, reads=hT_b[sb] + win_bufs(col), excl=[bkb])
        evac(bk, bkb)

    ln_pending = [None]

    def ln_tail():
        if ln_pending[0] is None:
            return
        gi, vi = ln_pending[0]
        ln_pending[0] = None
        op("dve", lambda q: q.scalar_tensor_tensor(out=nm4.t[gi][:], in0=mv.t[gi][:, :, 0], scalar=-1.0,
                                                   in1=rs4.t[gi][:], op0=ALU.mult, op1=ALU.mult),
           reads=mv_g[gi] + [rs4.b[gi]], writes=[nm4.b[gi]])
        for g in range(4):
            op("dve", lambda q, g=g: q.tensor_scalar(
                out=vhat.t[vi][:, g * P:(g + 1) * P], in0=gv.t[gi][:, g * P:(g + 1) * P],
                scalar1=rs4.t[gi][:, g:g + 1], scalar2=nm4.t[gi][:, g:g + 1], op0=ALU.mult, op1=ALU.add),
               reads=[gv.b[gi], rs4.b[gi], nm4.b[gi]], writes=[vhat_g[vi][g]])

    def stageB_groups(s, part):
        sb = s % 2
        groups = []
        if part == 0:
            for c in range(2):
                groups.append(lambda c=c: feat_chunk(s, C_Q + c * P, lambda bk, bkb: op(
                    "dve", lambda q: q.tensor_copy(out=qT.t[sb][:, c, :], in_=bk[:]),
                    writes=[qT_b[sb][c]], excl=[bkb])))
            groups.append(lambda: feat_chunk(s, C_K, lambda bk, bkb: op(
                "dve", lambda q: q.tensor_copy(out=kT.t[s % 3][:], in_=bk[:]), writes=[kT.b[s % 3]], excl=[bkb])))
            for c in range(2):
                groups.append(lambda c=c: feat_chunk(s, C_MQ + c * P, lambda bk, bkb: op(
                    "dve", lambda q: q.tensor_copy(out=mqT.t[sb][:, c, :], in_=bk[:]),
                    writes=[mqT_b[sb][c]], excl=[bkb])))

            def g_sv():
                bk, bkb = nb()

                def mmsv(q):
                    r = None
                    for t in range(4):
                        for k in range(8):
                            r = q.matmul(out=bk[:, t * P:(t + 1) * P], lhsT=hT.t[sb][:, k, t * P:(t + 1) * P],
                                         rhs=win_bf[:, k, C_SV:C_SV + P], start=(k == 0), stop=(k == 7))
                    return r
                op("pe", mmsv, reads=hT_b[sb] + win_bufs(C_SV), excl=[bkb])
                op("dve", lambda q: q.tensor_copy(
                    out=Vaug.t[s % 3][:, :, :, 0:64], in_=bk[:].rearrange("p (t h d) -> p t h d", t=4, h=2)),
                   writes=[Vaug.b[s % 3]], excl=[bkb])
            groups.append(g_sv)
        elif part == 1:
            def g_v(t):
                n = s * 4 + t
                gi = n % 2
                vi = n % 8
                bk, bkb = nb()

                def mm(q):
                    r = None
                    for k in range(8):
                        r = q.matmul(out=bk[:], lhsT=hT.t[sb][:, k, t * P:(t + 1) * P], rhs=win_bf[:, k, C_V:C_V + 512],
                                     start=(k == 0), stop=(k == 7))
                    return r
                op("pe", mm, reads=[hT_b[sb][t]] + win_bufs(C_V), excl=[bkb])
                op("act", lambda q: q.activation(out=gv.t[gi][:], in_=bk[:], func=AF.Gelu_apprx_tanh),
                   writes=[gv.b[gi]], excl=[bkb])
                for g in range(4):
                    op("dve", lambda q, g=g: q.bn_stats(out=st.t[gi][:, g, :], in_=gv.t[gi][:, g * P:(g + 1) * P]),
                       reads=[gv.b[gi]], writes=[st_g[gi][g]])
                for g in range(4):
                    op("dve", lambda q, g=g: q.bn_aggr(out=mv.t[gi][:, g, :], in_=st.t[gi][:, g, :]),
                       reads=[st_g[gi][g]], writes=[mv_g[gi][g]])
                op("pool", lambda q: q.tensor_scalar(out=rs4.t[gi][:], in0=mv.t[gi][:, :, 1], scalar1=EPS,
                                                     scalar2=None, op0=ALU.add),
                   reads=mv_g[gi], writes=[rs4.b[gi]])
                op("pool", lambda q: q.tensor_tensor(out=rs4.t[gi][:], in0=rs4.t[gi][:], in1=mhalf[:], op=ALU.pow),
                   reads=[rs4.b[gi], mhalf_b], writes=[rs4.b[gi]])
                ln_tail()
                ln_pending[0] = (gi, vi)

            def g_u(c):
                feat_chunk(s, C_U + c * P, lambda bk, bkb: op(
                    "act", lambda q: q.activation(out=gu.t[sb][:, c, :], in_=bk[:], func=AF.Gelu_apprx_tanh),
                    writes=[gu_b[sb][c]], excl=[bkb]))
                ln_tail()
            for t in range(4):
                groups.append(lambda t=t: g_v(t))
            for c in range(4):
                groups.append(lambda c=c: g_u(c))
        elif part == 2:
            def g_z(c):
                def ev(bk, bkb):
                    zi = c % 2
                    op("act", lambda q: q.activation(out=sztmp.t[zi][:], in_=bk[:], func=AF.Silu),
                       writes=[sztmp.b[zi]], excl=[bkb])
                    op("dve", lambda q: q.tensor_tensor(out=gu.t[sb][:, c, :], in0=gu.t[sb][:, c, :], in1=sztmp.t[zi][:],
                                                        op=ALU.mult),
                       reads=[sztmp.b[zi], gu_b[sb][c]], writes=[gu_b[sb][c]])
                feat_chunk(s, C_Z + c * P, ev)
            for c in range(4):
                groups.append(lambda c=c: g_z(c))
        else:
            for c in range(4):
                groups.append(lambda c=c: feat_chunk(s, C_Z + (4 + c) * P, lambda bk, bkb: op(
                    "act", lambda q: q.activation(out=sz47.t[sb][:, c, :], in_=bk[:], func=AF.Silu),
                    writes=[sz47_b[sb][c]], excl=[bkb])))
        return groups

    def stageB_part(s, part):
        for g in stageB_groups(s, part):
            g()

    e_ctr = [0]
    swa_E = {}

    def swa_logits(n):
        s, t = divmod(n, 4)
        sb = s % 2
        has_prev = (n % 16) != 0

        def grp(kv):
            bk, bkb = nb()
            bv = bk[:].rearrange("p (j g c) -> p j g c", j=2, g=2)
            ps = slice(kv * 64, (kv + 1) * 64)
            rhs = qT.t[sb][ps, :, t * P:(t + 1) * P]
            if t > 0:
                kprev, kprev_b = kT.t[s % 3][ps, (t - 1) * P:t * P], kT.b[s % 3]
            else:
                kprev, kprev_b = kT.t[(s - 1) % 3][ps, 3 * P:4 * P], kT.b[(s - 1) % 3]

            def mm(q):
                if has_prev:
                    q.matmul(out=bv[:, 0, :, :], lhsT=kprev, rhs=rhs, start=True, stop=True)
                return q.matmul(out=bv[:, 1, :, :], lhsT=kT.t[s % 3][ps, t * P:(t + 1) * P], rhs=rhs,
                                start=True, stop=True)
            op("pe", mm, reads=qT_b[sb] + [kT.b[s % 3], kprev_b], excl=[bkb])
            c0 = 0 if has_prev else 256
            e0i = e_ctr[0] % 2
            e_ctr[0] += 1
            ei = (n % 4) * 2 + kv
            op("act", lambda q: q.activation(out=E0.t[e0i][:, c0:512], in_=bk[:, c0:512], func=AF.Exp, scale=0.125),
               writes=[E0.b[e0i]], excl=[bkb])
            op("dve", lambda q: q.tensor_tensor(
                out=E.t[ei][:, c0:512], in0=E0.t[e0i][:, c0:512], in1=EBT[:, kv * 512 + c0:(kv + 1) * 512], op=ALU.mult),
               reads=[E0.b[e0i], EBT_b], writes=[E.b[ei]])
            swa_E[(n, kv)] = ei
        return [lambda: grp(0), lambda: grp(1)]

    def mem_logits(s):
        sb = s % 2
        b = s // 4

        def grp(c, mb):
            bks = [nb(), nb()]
            for hh in range(2):
                h = 2 * c + hh
                ps = slice(hh * 64, hh * 64 + 64)
                bk, bkb = bks[hh]
                op("pe", lambda q, bk=bk, ps=ps: q.matmul(
                    out=bk[:], lhsT=mkT.t[b][ps, c, mb * P:(mb + 1) * P], rhs=mqT.t[sb][ps, c, :],
                    start=True, stop=True),
                   reads=[mkT_b[b][c], mqT_b[sb][c]], excl=[bkb])
            for hh in range(2):
                h = 2 * c + hh
                bk, bkb = bks[hh]
                op("act", lambda q, bk=bk, h=h: q.activation(out=Em[:, h, mb, :], in_=bk[:], func=AF.Exp, scale=0.125),
                   writes=[Em_b[h][mb]], excl=[bkb])
        return [lambda c=c, mb=mb: grp(c, mb) for c in range(2) for mb in range(2)]

    def stageC_rest(n):
        s, t = divmod(n, 4)
        sb = s % 2
        b = s // 4
        has_prev = (n % 16) != 0
        yi = n % 3
        ci = n % 2
        vi = n % 8

        def g_sp():
            bk, bkb = nb()

            def mmsp(q):
                r = None
                for g in range(4):
                    r = q.matmul(out=bk[:, g * P:(g + 1) * P], lhsT=vhat.t[vi][:, g * P:(g + 1) * P],
                                 rhs=wsT_bf[:, g * P:(g + 1) * P], start=True, stop=True)
                return r
            op("pe", mmsp, reads=vhat_g[vi] + [wsT_bf_b], excl=[bkb])
            for g in range(4):
                op("dve", lambda q, g=g: q.scalar_tensor_tensor(
                    out=tA.t[ci][:, g * P:(g + 1) * P], in0=bk[:, g * P:(g + 1) * P], scalar=vgcol[:, g:g + 1],
                    in1=biasA[:, g * P:(g + 1) * P], op0=ALU.mult, op1=ALU.add),
                   reads=[vgcol_b, biasA_b], writes=[tA_g[ci][g]], excl=[bkb])
            op("pool", lambda q: q.tensor_tensor(
                out=yT.t[yi][:, 0:4, :], in0=tA.t[ci][:].rearrange("p (g c) -> p g c", g=4),
                in1=gu.t[sb][:, :, t * P:(t + 1) * P], op=ALU.mult),
               reads=tA_g[ci] + gu_b[sb], writes=[yT_a[yi]])

        def g_pvs():
            bk, bkb = nb()
            pv = bk[:, 0:4 * VW].rearrange("p (h w) -> p h w", h=4)
            if t > 0:
                vprev, vprev_b = Vaug.t[s % 3][:, t - 1, :, :], Vaug.b[s % 3]
            else:
                vprev, vprev_b = Vaug.t[(s - 1) % 3][:, 3, :, :], Vaug.b[(s - 1) % 3]
            vcur = Vaug.t[s % 3][:, t, :, :]
            eis = [swa_E[(n, 0)], swa_E[(n, 1)]]

            def mmpv(q):
                r = None
                for kv in range(2):
                    ev = E.t[eis[kv]][:].rearrange("p (j g c) -> p j g c", j=2, g=2)
                    for g in range(2):
                        h = 2 * kv + g
                        if has_prev:
                            q.matmul(out=pv[:, h, 0:65], lhsT=ev[:, 0, g, :], rhs=vprev[:, kv, 0:65], start=True, stop=False)
                        r = q.matmul(out=pv[:, h, 0:65], lhsT=ev[:, 1, g, :], rhs=vcur[:, kv, 0:65],
                                     start=(not has_prev), stop=True)
                return r
            op("pe", mmpv, reads=[E.b[eis[0]], E.b[eis[1]], Vaug.b[s % 3], vprev_b], excl=[bkb])
            op("dve", lambda q: q.tensor_tensor(out=den.t[ci][:], in0=pv[:, :, 64], in1=esink[:], op=ALU.add),
               reads=[esink_b], writes=[den.b[ci]], excl=[bkb])
            op("dve", lambda q: q.reciprocal(out=rden.t[ci][:], in_=den.t[ci][:]), reads=[den.b[ci]], writes=[rden.b[ci]])
            op("dve", lambda q: q.tensor_tensor(
                out=ybc.t[ci][:, 0:256].rearrange("p (h d) -> p h d", h=4), in0=pv[:, :, 0:64],
                in1=rden.t[ci][:].unsqueeze(2).to_broadcast([P, 4, 64]), op=ALU.mult),
               reads=[rden.b[ci]], writes=[ybc_s[ci]], excl=[bkb])

        def g_pvm():
            bk, bkb = nb()
            pm = bk[:, 0:4 * VW].rearrange("p (h w) -> p h w", h=4)

            def mmpm(q):
                r = None
                for h in range(4):
                    for mb in range(2):
                        r = q.matmul(out=pm[:, h, 0:65], lhsT=Em[:, h, mb, t * P:(t + 1) * P],
                                     rhs=mvaug.t[b][:, mb, h, 0:65], start=(mb == 0), stop=(mb == 1))
                return r
            op("pe", mmpm, reads=[x for row in Em_b for x in row] + mvaug_b[b], excl=[bkb])
            op("dve", lambda q: q.reciprocal(out=rdm.t[ci][:], in_=pm[:, :, 64]), writes=[rdm.b[ci]], excl=[bkb])
            op("dve", lambda q: q.tensor_tensor(
                out=ybc.t[ci][:, 256:512].rearrange("p (h d) -> p h d", h=4), in0=pm[:, :, 0:64],
                in1=rdm.t[ci][:].unsqueeze(2).to_broadcast([P, 4, 64]), op=ALU.mult),
               reads=[rdm.b[ci]], writes=[ybc_m[ci]], excl=[bkb])
        return [g_sp, g_pvs, g_pvm]

    def stageC_tr(n):
        s, t = divmod(n, 4)
        sb = s % 2
        yi = n % 3
        ci = n % 2

        def grp():
            bk, bkb = nb()
            bkv = bk[:, 0:256].bitcast(BF16).rearrange("p (k c) -> p k c", k=4)

            def tr(q):
                r = None
                for c in range(4):
                    r = q.transpose(out=bkv[:, c, :], in_=ybc.t[ci][:, c * P:(c + 1) * P], identity=ident_bf[:])
                return r
            op("pe", tr, reads=[ybc_s[ci], ybc_m[ci], ident_bf_b], excl=[bkb])
            op("dve", lambda q: q.tensor_tensor(out=yT.t[yi][:, 4:8, :], in0=bkv, in1=sz47.t[sb][:, :, t * P:(t + 1) * P],
                                                op=ALU.mult),
               reads=sz47_b[sb], writes=[yT_b[yi]], excl=[bkb])
        return [grp]

    def D_load(n):
        di = n % 2
        ft, fb = FB.t[XD[di]], FB.b[XD[di]]
        op("sp", lambda q: q.dma_start(out=ft[:], in_=x_d[n * P:(n + 1) * P, :]), writes=[fb], dma_key=f"F{XD[di]}")

    def stageD(n):
        yi = n % 3
        di = n % 2
        tOt = FB.t[TO]

        def grp(half):
            bk, bkb = nb()
            hs = slice(half * 512, (half + 1) * 512)

            def mm(q):
                r = None
                for k in range(8):
                    r = q.matmul(out=bk[:], lhsT=yT.t[yi][:, k, :], rhs=wout_bf[:, k, hs], start=(k == 0), stop=(k == 7))
                return r
            op("pe", mm, reads=[yT_a[yi], yT_b[yi], wout_b[half]], excl=[bkb])
            op("act", lambda q: q.activation(out=junk[:], in_=bk[:], func=AF.Square,
                                             accum_out=ss2.t[di][:, half:half + 1]),
               writes=[ss2_b[di][half], junk_b], excl=[bkb])
            op("dve", lambda q: q.tensor_tensor(out=tOt[:, hs], in0=bk[:], in1=gpost[:, hs], op=ALU.mult),
               reads=[gpost_b], writes=[tO_b[half]], excl=[bkb])
            if half == 1:
                op("pool", lambda q: q.tensor_scalar(out=r2.t[di][:], in0=ss2.t[di][:, 0:1], scalar1=ss2.t[di][:, 1:2],
                                                     scalar2=D * EPS, op0=ALU.add, op1=ALU.add),
                   reads=ss2_b[di], writes=[r2.b[di]])
                op("pool", lambda q: q.tensor_tensor(out=r2.t[di][:], in0=r2.t[di][:], in1=mhalf[:, 0:1], op=ALU.pow),
                   reads=[r2.b[di], mhalf_b], writes=[r2.b[di]])
                op("pool", lambda q: q.tensor_scalar(out=r2.t[di][:], in0=r2.t[di][:], scalar1=32.0, scalar2=None,
                                                     op0=ALU.mult),
                   reads=[r2.b[di]], writes=[r2.b[di]])
        return [lambda: grp(0), lambda: grp(1)]

    def D_fin(n):
        di = n % 2
        ft, fb = FB.t[XD[di]], FB.b[XD[di]]
        tOt = FB.t[TO]
        if n >= NT - 6:
            op("act", lambda q: q.activation(out=tOt[:], in_=tOt[:], func=AF.Copy, scale=r2.t[di][:, 0:1]),
               reads=tO_b + [r2.b[di]], writes=tO_b)
            op("pool", lambda q: q.tensor_tensor(out=ft[:], in0=tOt[:], in1=ft[:], op=ALU.add),
               reads=tO_b + [fb], writes=[fb])
        else:
            op("dve", lambda q: q.scalar_tensor_tensor(out=ft[:], in0=tOt[:], scalar=r2.t[di][:, 0:1], in1=ft[:],
                                                       op0=ALU.mult, op1=ALU.add),
               reads=tO_b + [r2.b[di], fb], writes=[fb])
        op("sp", lambda q: q.dma_start(out=out_d[n * P:(n + 1) * P, :], in_=ft[:]), reads=[fb],
           dma_key=f"O{di}", is_store=True)

    a_slot = {}
    a_xs = {}

    def A_ld(n):
        if 0 <= n < NT:
            a_slot[n] = A_load(x_d[n * P:(n + 1) * P, :])

    def A_pr(n):
        if 0 <= n < NT:
            a_xs[n] = A_pre(a_slot[n])

    def A_pr2(n):
        if 0 <= n < NT:
            A_pre2(a_slot[n], a_xs[n])

    def A_t(n):
        if 0 <= n < NT:
            s_, t_ = divmod(n, 4)
            A_tr(a_xs[n], gpre, gpre_b, hT.t[s_ % 2], t_ * P, hT_b[s_ % 2][t_])

    A_ld(0)
    A_ld(1)
    A_ld(2)
    a_slot[3] = A_load(x_d[3 * P:4 * P, :], force=3)
    load_win_piece(0)
    load_win_piece(1)
    load_win_piece(3)
    load_win_piece(2)
    load_wmem()
    setup_first()
    A_pr(0)
    A_pr2(0)
    A_pr(1)
    A_pr2(1)
    A_t(0)
    A_pr(2)
    A_pr2(2)
    A_t(1)
    A_pr(3)
    A_pr2(3)
    A_t(2)
    A_t(3)
    load_win_piece(4)
    load_win_piece(5)
    load_wout()
    mem_pre(0)
    setup_rest()
    stageB_part(0, 0)
    stageB_part(0, 1)
    mem_tr(0)
    mem_pre(1)
    mem_kv(0)
    setup_compute()
    load_const(gpost[:], gpost_d, gpost_b)
    A_ld(4)
    mem_tr(1)
    A_ld(5)
    A_pr(4)
    A_pr2(4)
    A_ld(6)
    A_pr(5)
    A_pr2(5)
    mem_kv(1)
    setup_ebt()
    A_t(4)
    A_ld(7)
    A_pr(6)
    A_pr2(6)
    A_t(5)
    stageB_part(0, 2)
    A_ld(8)
    A_pr(7)
    A_pr2(7)
    A_t(6)
    stageB_part(0, 3)
    A_t(7)
    def interleave(a, b):
        out = []
        for i in range(max(len(a), len(b))):
            if i < len(a):
                out.append(a[i])
            if i < len(b):
                out.append(b[i])
        return out

    for q in range(-2, NT + 2):
        A_ld(q + 11)
        if 0 <= q - 1 < NT:
            D_load(q - 1)
        A_pr(q + 10)
        cg = stageC_rest(q + 1) if 0 <= q + 1 < NT else []
        sg = []
        mg = []
        if q % 2 == 0 and 0 <= q + 2 < NT:
            sg = swa_logits(q + 2) + swa_logits(q + 3)
            if (q + 2) % 4 == 0:
                mg = mem_logits((q + 2) // 4)
        bg = stageB_groups((q + 6) // 4, (q + 6) % 4) if 0 <= q + 6 < NT else []
        dg = stageD(q - 1) if 0 <= q - 1 < NT else []
        tg = stageC_tr(q) if 0 <= q < NT else []
        ag = [lambda q=q: A_t(q + 10)] if 0 <= q + 10 < NT else []
        part = (q + 6) % 4
        if mg:
            short = interleave(cg, sg) + mg
            long_ = dg + bg + tg + ag
        elif sg:
            short = sg + cg
            long_ = dg + tg + ag + bg
        else:
            short = cg
            long_ = bg[:len(bg) // 2] + dg + bg[len(bg) // 2:] + tg + ag
        if mg:
            seq = []
            si = 0
            for lg in long_:
                seq += short[si:si + 2]
                si += 2
                seq.append(lg)
            seq += short[si:]
        else:
            seq = interleave(short, long_)
        if 0 <= q - 2 < NT:
            D_fin(q - 2)
        for i, g in enumerate(seq):
            g()
            if i == 0:
                A_pr2(q + 10)
        if not seq:
            A_pr2(q + 10)
        if 0 <= q + 6 < NT and part == 1 and q + 6 + 4 >= NT:
            ln_tail()
    pg.finish()
    pg.emit(nc)
    return nc


def _t5_buckets(dist):
    n = np.maximum(dist, 0)
    max_exact = 16
    large = max_exact + (np.log(np.maximum(n, 1) / max_exact) / np.log(128 / max_exact) * (32 - max_exact)).astype(np.int32)
    large = np.minimum(large, 31)
    return np.where(n < max_exact, n, large).astype(np.int32)


def _prep_shared(pre_norm_g, post_norm_g, mem_norm_g, w_in, w_mem_kv, v_norm_g, v_norm_b, w_spatial, b_spatial,
                 attn_sinks, rel_bias, w_out):
    f = lambda a: np.ascontiguousarray(np.asarray(a, dtype=np.float32))
    w = np.asarray(w_in, dtype=np.float32)[0]
    sq = 1024
    cols = np.concatenate([
        np.arange(0, 512),
        sq + np.concatenate([np.arange(0, 64), np.arange(128, 192), np.arange(64, 128), np.arange(192, 256)]),
        np.arange(1280, 1408),
        np.arange(1536, 1792),
        np.arange(1792, 2816),
        np.arange(512, 1024),
        np.arange(1408, 1536),
    ])
    d = {}
    d["win"] = f(w[:, cols])
    d["wout"] = f(np.asarray(w_out)[0])
    d["wmem"] = f(np.asarray(w_mem_kv)[0])
    d["gpre"] = f(np.asarray(pre_norm_g)[0].reshape(8, P).T)
    d["gmem"] = f(np.asarray(mem_norm_g)[0].reshape(8, P).T)
    d["gpost"] = f(np.broadcast_to(np.asarray(post_norm_g)[0][None, :], (P, D)))
    d["vgcol"] = f(np.asarray(v_norm_g)[0].reshape(4, P).T)
    d["vbb"] = f(np.broadcast_to(np.asarray(v_norm_b)[0][None, :], (P, 512)))
    d["wsT"] = f(np.asarray(w_spatial)[0].transpose(2, 0, 1).reshape(P, 512))
    sidx = np.arange(P)[:, None]
    tidx = np.arange(P)[None, :]
    cm = (tidx >= sidx).astype(np.float32)
    d["cmask"] = f(np.broadcast_to(cm[:, None, :], (P, 4, P)).reshape(P, 512))
    d["bsrow"] = f(np.asarray(b_spatial)[0].reshape(1, 512))
    jp = np.arange(P)[:, None, None]
    jb = np.arange(2)[None, :, None]
    tq = np.arange(P)[None, None, :]
    dist = tq + P - (jb * P + jp)
    valid = (dist >= 0) & (dist < 128)
    bk = _t5_buckets(dist)
    rb = np.asarray(rel_bias, dtype=np.float32)
    eb = np.zeros((P, 2, 2, 2, P), np.float32)
    for kv in range(2):
        for g in range(2):
            eb[:, kv, :, g, :] = rb[bk, 2 * kv + g]
    d["ebias"] = f(eb.reshape(P, 1024))
    em = np.broadcast_to(valid[:, None, :, None, :], (P, 2, 2, 2, P)).astype(np.float32)
    d["emask"] = f(em.reshape(P, 1024))
    d["sinks"] = f(np.broadcast_to(np.asarray(attn_sinks)[0][None, :], (P, 4)))
    d["ident"] = np.eye(P, dtype=np.float32)
    return d


_NC_CACHE = {}


def kernel(x, mem, pre_norm_g, post_norm_g, mem_norm_g, w_in, w_mem_kv, v_norm_g, v_norm_b,
           w_spatial, b_spatial, attn_sinks, rel_bias, w_out):
    x = np.asarray(x, dtype=np.float32)
    mem = np.asarray(mem, dtype=np.float32)
    shared = _prep_shared(pre_norm_g, post_norm_g, mem_norm_g, w_in, w_mem_kv, v_norm_g, v_norm_b, w_spatial,
                          b_spatial, attn_sinks, rel_bias, w_out)
    n = 8
    in_maps = []
    for c in range(n):
        m = dict(shared)
        m["x"] = np.ascontiguousarray(x[2 * c:2 * c + 2].reshape(NT * P, D))
        m["mem"] = np.ascontiguousarray(mem[2 * c:2 * c + 2].reshape(512, D))
        in_maps.append(m)
    if "nc" not in _NC_CACHE:
        _NC_CACHE["nc"] = build_program()
    nc = _NC_CACHE["nc"]
    res = run_bass_kernel_spmd(nc, in_maps, core_ids=list(range(n)))
    out = np.stack([np.asarray(r["out"], dtype=np.float32).reshape(2, 2048, D) for r in res.results], axis=0)
    return out.reshape(16, 2048, D)
```

```python
import numpy as np
import concourse.bass as bass
import concourse.mybir as mybir
from concourse.bass_utils import run_bass_kernel_spmd

F32 = mybir.dt.float32
BF16 = mybir.dt.bfloat16
AF = mybir.ActivationFunctionType
ALU = mybir.AluOpType

P = 128
NT = 32
D = 1024
INW = 2816
EPS = 1e-6
NEG = -30000.0
VW = 68
C_U, C_Q, C_K, C_MQ, C_Z, C_V, C_SV = 0, 512, 768, 896, 1152, 2176, 2688


class Buf:
    __slots__ = ("name", "w", "r")

    def __init__(self, name):
        self.name = name
        self.w = None
        self.r = []


class Ins:
    __slots__ = ("eng", "fn", "deps", "key", "val", "need_inc", "is_dma", "final")


class Prog:
    def __init__(self):
        self.ins = []
        self.store_ids = []

    def op(self, eng, fn, reads=(), writes=(), excl=(), dma_key=None, final=False, is_store=False):
        i = len(self.ins)
        cand = set()
        for b in reads:
            if b.w is not None:
                cand.add(b.w)
        for b in writes:
            if b.w is not None:
                cand.add(b.w)
            cand.update(b.r)
        deps = set()
        for d in cand:
            di = self.ins[d]
            if di.is_dma or dma_key is not None or di.eng != eng or eng != "pe":
                deps.add(d)
        for b in excl:
            if b.w is not None and self.ins[b.w].eng != eng:
                deps.add(b.w)
        for b in reads:
            b.r.append(i)
        for b in writes:
            b.w = i
            b.r = []
        for b in excl:
            b.w = i
        x = Ins()
        x.eng = eng
        x.fn = fn
        x.deps = deps
        x.is_dma = dma_key is not None
        x.key = dma_key if dma_key is not None else eng
        x.val = None
        x.need_inc = x.is_dma
        x.final = final
        self.ins.append(x)
        if is_store:
            self.store_ids.append(i)
        return i

    def finish(self):
        x = Ins()
        x.eng = "sp"
        x.fn = None
        x.deps = set(self.store_ids)
        x.is_dma = False
        x.key = "sp"
        x.val = None
        x.need_inc = False
        x.final = False
        self.ins.append(x)

    def emit(self, nc):
        ins = self.ins
        for x in ins:
            for d in x.deps:
                ins[d].need_inc = True
        counts = {}
        for x in ins:
            if x.need_inc:
                step = 16 if x.is_dma else 1
                counts[x.key] = counts.get(x.key, 0) + step
                x.val = counts[x.key]
        sems = {k: nc.alloc_semaphore("s_" + str(k)) for k in counts}
        by_eng = {e: [] for e in ("pe", "act", "dve", "pool", "sp")}
        for x in ins:
            by_eng[x.eng].append(x)

        def run(eng_name, e):
            waited = {}
            for x in by_eng[eng_name]:
                need = {}
                for d in x.deps:
                    di = ins[d]
                    v = counts[di.key] if di.final else di.val
                    if v > need.get(di.key, 0):
                        need[di.key] = v
                for k, v in need.items():
                    if waited.get(k, 0) < v:
                        e.wait_ge(sems[k], v)
                        waited[k] = v
                if x.fn is None:
                    continue
                r = x.fn(e)
                if x.need_inc:
                    r.then_inc(sems[x.key], 16 if x.is_dma else 1)

        with nc.Block() as block:
            @block.tensor
            def _(e):
                run("pe", e)

            @block.scalar
            def _(e):
                run("act", e)

            @block.vector
            def _(e):
                run("dve", e)

            @block.gpsimd
            def _(e):
                run("pool", e)

            @block.sync
            def _(e):
                run("sp", e)


class Ring:
    def __init__(self, nc, name, shape, dt, n):
        self.t = [nc.alloc_sbuf_tensor(f"sb_{name}{i}", list(shape), dt) for i in range(n)]
        self.b = [Buf(f"{name}{i}") for i in range(n)]
        self.n = n


def build_program():
    nc = bass.Bass("TRN2", target_bir_lowering=False)
    pg = Prog()
    op = pg.op

    def din(name, shape):
        return nc.dram_tensor(name, list(shape), F32, kind="ExternalInput").ap()

    x_d = din("x", [NT * P, D])
    mem_d = din("mem", [512, D])
    win_d = din("win", [D, INW])
    wout_d = din("wout", [D, D])
    wmem_d = din("wmem", [D, 512])
    gpre_d = din("gpre", [P, 8])
    gmem_d = din("gmem", [P, 8])
    gpost_d = din("gpost", [P, D])
    vgcol_d = din("vgcol", [P, 4])
    vbb_d = din("vbb", [P, 512])
    wsT_d = din("wsT", [P, 512])
    cmask_d = din("cmask", [P, 512])
    bsrow_d = din("bsrow", [1, 512])
    ebias_d = din("ebias", [P, 1024])
    emask_d = din("emask", [P, 1024])
    sinks_d = din("sinks", [P, 4])
    ident_d = din("ident", [P, P])
    out_d = nc.dram_tensor("out", [NT * P, D], F32, kind="ExternalOutput").ap()

    def S(name, shape, dt=F32):
        return nc.alloc_sbuf_tensor("sb_" + name, list(shape), dt), Buf(name)

    FB = Ring(nc, "F", [P, D], F32, 6)
    XA = (0, 1, 2, 5)
    XD = (3, 4)
    TO = 5
    win_bf = nc.alloc_sbuf_tensor("sb_win_bf", [P, 8, INW], BF16)
    wout_bf = nc.alloc_sbuf_tensor("sb_wout_bf", [P, 8, D], BF16)
    wmem_bf = nc.alloc_sbuf_tensor("sb_wmem_bf", [P, 8, 512], BF16)
    PIECES = [(512, 1152), (2688, 2816), (0, 512), (2176, 2688), (1152, 1664), (1664, 2176)]
    win_b = [[Buf(f"win{k}_{j}") for j in range(len(PIECES))] for k in range(8)]

    def win_bufs(col):
        for j, (c0, c1) in enumerate(PIECES):
            if c0 <= col < c1:
                return [win_b[k][j] for k in range(8)]
        raise ValueError(col)
    wout_b = [Buf(f"wout{h}") for h in range(2)]
    wmem_b = [Buf(f"wmem{k}") for k in range(8)]

    ident_bf, ident_bf_b = S("ident_bf", [P, P], BF16)
    gpre, gpre_b = S("gpre", [P, 8])
    gmem, gmem_b = S("gmem", [P, 8])
    gpost, gpost_b = S("gpost", [P, D])
    vgcol, vgcol_b = S("vgcol", [P, 4])
    wsT_bf, wsT_bf_b = S("wsT_bf", [P, 512], BF16)
    biasA, biasA_b = S("biasA", [P, 512])
    EBT, EBT_b = S("EBT", [P, 1024])
    esink, esink_b = S("esink", [P, 4])
    mhalf, mhalf_b = S("mhalf", [P, 4])
    junk, junk_b = S("junk", [P, 512], BF16)

    ssA = Ring(nc, "ssA", [P, 1], F32, 2)
    rrA = Ring(nc, "rrA", [P, 1], F32, 2)
    xs = Ring(nc, "xs", [P, D], BF16, 2)
    hT = Ring(nc, "hT", [P, 8, 512], BF16, 2)
    hT_b = [[Buf(f"hT{i}_{t}") for t in range(4)] for i in range(2)]
    gu = Ring(nc, "gu", [P, 4, 512], BF16, 2)
    gu_b = [[Buf(f"gu{i}_{c}") for c in range(4)] for i in range(2)]
    sz47 = Ring(nc, "sz47", [P, 4, 512], BF16, 2)
    sz47_b = [[Buf(f"sz47{i}_{c}") for c in range(4)] for i in range(2)]
    sztmp = Ring(nc, "sztmp", [P, 512], BF16, 2)
    qT = Ring(nc, "qT", [P, 2, 512], BF16, 2)
    qT_b = [[Buf(f"qT{i}_{c}") for c in range(2)] for i in range(2)]
    kT = Ring(nc, "kT", [P, 512], BF16, 3)
    mqT = Ring(nc, "mqT", [P, 2, 512], BF16, 2)
    mqT_b = [[Buf(f"mqT{i}_{c}") for c in range(2)] for i in range(2)]
    Vaug = Ring(nc, "Vaug", [P, 4, 2, VW], BF16, 3)
    gv = Ring(nc, "gv", [P, 512], F32, 2)
    st = Ring(nc, "st", [P, 4, 6], F32, 2)
    mv = Ring(nc, "mv", [P, 4, 2], F32, 2)
    rs4 = Ring(nc, "rs4", [P, 4], F32, 2)
    nm4 = Ring(nc, "nm4", [P, 4], F32, 2)
    vhat = Ring(nc, "vhat", [P, 512], BF16, 8)
    tA = Ring(nc, "tA", [P, 512], F32, 2)
    st_g = [[Buf(f"st{i}_{g}") for g in range(4)] for i in range(2)]
    mv_g = [[Buf(f"mv{i}_{g}") for g in range(4)] for i in range(2)]
    vhat_g = [[Buf(f"vhat{i}_{g}") for g in range(4)] for i in range(8)]
    tA_g = [[Buf(f"tA{i}_{g}") for g in range(4)] for i in range(2)]
    E0 = Ring(nc, "E0", [P, 512], F32, 2)
    E = Ring(nc, "E", [P, 512], BF16, 8)
    Em, _ = S("Em", [P, 4, 2, 512], BF16)
    Em_b = [[Buf(f"Em{h}_{m}") for m in range(2)] for h in range(4)]
    den = Ring(nc, "den", [P, 4], F32, 2)
    rden = Ring(nc, "rden", [P, 4], F32, 2)
    rdm = Ring(nc, "rdm", [P, 4], F32, 2)
    ybc = Ring(nc, "ybc", [P, 512], BF16, 2)
    ybc_s = [Buf("ybcs0"), Buf("ybcs1")]
    ybc_m = [Buf("ybcm0"), Buf("ybcm1")]
    yT = Ring(nc, "yT", [P, 8, P], BF16, 3)
    yT_a = [Buf(f"yTa{i}") for i in range(3)]
    yT_b = [Buf(f"yTb{i}") for i in range(3)]
    ss2 = Ring(nc, "ss2", [P, 2], F32, 2)
    ss2_b = [[Buf(f"ss2{i}_{h}") for h in range(2)] for i in range(2)]
    r2 = Ring(nc, "r2", [P, 1], F32, 2)
    tO_b = [Buf("tO_h0"), Buf("tO_h1")]
    mkT = Ring(nc, "mkT", [P, 2, 256], BF16, 2)
    mkT_b = [[Buf(f"mkT{i}_{c}") for c in range(2)] for i in range(2)]
    mvaug = Ring(nc, "mvaug", [P, 2, 4, VW], BF16, 2)
    mvaug_b = [[Buf(f"mvaug{i}_{c}") for c in range(2)] for i in range(2)]

    banks = [nc.alloc_psum_tensor(f"bk{i}", [P, 512], F32) for i in range(8)]
    bank_b = [Buf(f"bk{i}") for i in range(8)]
    bank_ctr = [0]

    def nb():
        i = bank_ctr[0] % 8
        bank_ctr[0] += 1
        return banks[i], bank_b[i]

    def load_const(dst, src, b):
        op("sp", lambda q: q.dma_start(out=dst, in_=src), writes=[b], dma_key="c_" + b.name)

    ident_f = FB.t[3][:, 640:768]
    bsrow = FB.t[3]
    ones_row_t, ones_row_b = S("ones_row", [1, P])
    ones_row = ones_row_t[0:1, :]
    vbb_t = FB.t[4][:, 0:512]
    wsm = FB.t[4][:, 512:1024]
    wsm_b = FB.b[4]

    def setup_first():
        op("pool", lambda q: q.memset(mhalf[:], -0.5), writes=[mhalf_b])
        op("pool", lambda q: q.memset(ones_row, 1.0), writes=[ones_row_b])
        for i in range(3):
            op("pool", lambda q, i=i: q.memset(Vaug.t[i][:], 1.0), writes=[Vaug.b[i]])
        for i in range(2):
            op("pool", lambda q, i=i: q.memset(mvaug.t[i][:], 1.0), writes=mvaug_b[i])
        op("sp", lambda q: q.dma_start(out=FB.t[3][:, 640:768], in_=ident_d), writes=[FB.b[3]], dma_key="F3")
        load_const(gpre[:], gpre_d, gpre_b)
        load_const(gmem[:], gmem_d, gmem_b)
        op("dve", lambda q: q.tensor_copy(out=ident_bf[:], in_=ident_f), reads=[FB.b[3]], writes=[ident_bf_b])
        op("sp", lambda q: q.dma_start(out=gpost[:, 0:512], in_=wsT_d), writes=[gpost_b], dma_key="c_gpost")
        op("sp", lambda q: q.dma_start(out=gpost[:, 512:1024], in_=cmask_d), writes=[gpost_b], dma_key="c_gpost")
        op("sp", lambda q: q.dma_start(out=vbb_t, in_=vbb_d), writes=[FB.b[4]], dma_key="F4")
        op("sp", lambda q: q.dma_start(out=FB.t[3][0:1, 0:512], in_=bsrow_d), reads=[FB.b[3]], writes=[FB.b[3]],
           dma_key="F3")

    def setup_rest():
        load_const(vgcol[:], vgcol_d, vgcol_b)
        load_const(esink[:], sinks_d, esink_b)
        load_const(EBT[:], ebias_d, EBT_b)
        for hh in range(2):
            op("sp", lambda q, hh=hh: q.dma_start(out=tA.t[hh][:], in_=emask_d[:, hh * 512:(hh + 1) * 512]),
               writes=tA_g[hh], dma_key=f"tA{hh}")

    def setup_compute():
        op("dve", lambda q: q.tensor_tensor(out=wsm, in0=gpost[:, 0:512], in1=gpost[:, 512:1024], op=ALU.mult),
           reads=[gpost_b, FB.b[4]], writes=[wsm_b])
        op("dve", lambda q: q.tensor_copy(out=wsT_bf[:], in_=wsm), reads=[wsm_b], writes=[wsT_bf_b])
        bk, bkb = nb()

        def bias_mm(q):
            r = None
            for g in range(4):
                q.matmul(out=bk[:, g * P:(g + 1) * P], lhsT=vbb_t[:, g * P:(g + 1) * P], rhs=wsm[:, g * P:(g + 1) * P],
                         start=True, stop=False)
                r = q.matmul(out=bk[:, g * P:(g + 1) * P], lhsT=ones_row, rhs=bsrow[0:1, g * P:(g + 1) * P],
                             start=False, stop=True)
            return r
        op("pe", bias_mm, reads=[wsm_b, FB.b[3], ones_row_b], excl=[bkb])
        op("dve", lambda q: q.tensor_copy(out=biasA[:], in_=bk[:]), writes=[biasA_b], excl=[bkb])

    def setup_ebt():
        op("act", lambda q: q.activation(out=EBT[:], in_=EBT[:], func=AF.Exp), reads=[EBT_b], writes=[EBT_b])
        for hh in range(2):
            op("dve", lambda q, hh=hh: q.tensor_tensor(out=EBT[:, hh * 512:(hh + 1) * 512],
                                                       in0=EBT[:, hh * 512:(hh + 1) * 512], in1=tA.t[hh][:], op=ALU.mult),
               reads=[EBT_b] + tA_g[hh], writes=[EBT_b])
        op("act", lambda q: q.activation(out=esink[:], in_=esink[:], func=AF.Exp), reads=[esink_b], writes=[esink_b])

    win_v = win_d.rearrange("(k p) c -> p k c", p=P)
    wout_v = wout_d.rearrange("(k p) c -> p k c", p=P)
    wmem_v = wmem_d.rearrange("(k p) c -> p k c", p=P)

    def load_win_piece(j):
        c0, c1 = PIECES[j]
        op("pool", lambda q: q.dma_start(out=win_bf[:, :, c0:c1], in_=win_v[:, :, c0:c1]),
           writes=[win_b[k][j] for k in range(8)], dma_key=f"W{j}")

    def load_wmem():
        op("pool", lambda q: q.dma_start(out=wmem_bf[:], in_=wmem_v), writes=wmem_b, dma_key="Wm")

    def load_wout():
        for h in range(2):
            op("pool", lambda q, h=h: q.dma_start(out=wout_bf[:, :, h * 512:(h + 1) * 512],
                                                  in_=wout_v[:, :, h * 512:(h + 1) * 512]),
               writes=[wout_b[h]], dma_key=f"Wo{h}")

    a_ctr = [0]

    xa_live = set()

    def A_load(src_ap, force=None):
        if force is not None:
            ft, fb = FB.t[XA[force]], FB.b[XA[force]]
            xa_live.add(force)
            op("sp", lambda q: q.dma_start(out=ft[:], in_=src_ap), writes=[fb], dma_key=f"F{XA[force]}")
            return force
        for d in range(3):
            i = (a_ctr[0] + d) % 3
            if i not in xa_live:
                break
        else:
            raise RuntimeError("no free XA slot")
        a_ctr[0] = i + 1
        xa_live.add(i)
        ft, fb = FB.t[XA[i]], FB.b[XA[i]]
        op("sp", lambda q: q.dma_start(out=ft[:], in_=src_ap), writes=[fb], dma_key=f"F{XA[i]}")
        return i

    ac_ctr = [0]

    def A_pre(i):
        ft, fb = FB.t[XA[i]], FB.b[XA[i]]
        j = ac_ctr[0] % 2
        ac_ctr[0] += 1
        op("act", lambda q: q.activation(out=xs.t[j][:], in_=ft[:], func=AF.Square, accum_out=ssA.t[j][:]),
           reads=[fb], writes=[xs.b[j], ssA.b[j]])
        op("pool", lambda q: q.tensor_scalar(out=rrA.t[j][:], in0=ssA.t[j][:], scalar1=1.0 / D, scalar2=EPS,
                                             op0=ALU.mult, op1=ALU.add), reads=[ssA.b[j]], writes=[rrA.b[j]])
        op("pool", lambda q: q.tensor_tensor(out=rrA.t[j][:], in0=rrA.t[j][:], in1=mhalf[:, 0:1], op=ALU.pow),
           reads=[rrA.b[j], mhalf_b], writes=[rrA.b[j]])
        return j

    def A_pre2(i, j):
        ft, fb = FB.t[XA[i]], FB.b[XA[i]]
        xa_live.discard(i)
        op("dve", lambda q: q.tensor_scalar(out=xs.t[j][:], in0=ft[:], scalar1=rrA.t[j][:, 0:1], scalar2=None,
                                            op0=ALU.mult), reads=[fb, rrA.b[j]], writes=[xs.b[j]])

    def A_tr(j, gam, gam_b, dst_t, dst_col, dst_buf):
        bk, bkb = nb()
        bkv = bk[:].bitcast(BF16).rearrange("p (k c) -> p k c", k=8)

        def tr(q):
            r = None
            for k in range(8):
                r = q.transpose(out=bkv[:, k, :], in_=xs.t[j][:, k * P:(k + 1) * P], identity=ident_bf[:])
            return r
        op("pe", tr, reads=[xs.b[j], ident_bf_b], excl=[bkb])
        op("dve", lambda q: q.tensor_tensor(out=dst_t[:, :, dst_col:dst_col + P], in0=bkv,
                                            in1=gam[:].unsqueeze(2).to_broadcast([P, 8, P]), op=ALU.mult),
           reads=[gam_b], writes=[dst_buf], excl=[bkb])

    def stageA(src_ap, gam, gam_b, dst_t, dst_col, dst_buf):
        i = A_load(src_ap)
        j = A_pre(i)
        A_pre2(i, j)
        A_tr(j, gam, gam_b, dst_t, dst_col, dst_buf)

    mem_xs = {}

    def mem_pre(b):
        for mt in range(2):
            i = A_load(mem_d[(b * 2 + mt) * P:(b * 2 + mt + 1) * P, :])
            j = A_pre(i)
            A_pre2(i, j)
            mem_xs[(b, mt)] = j

    def mem_tr(b):
        for mt in range(2):
            A_tr(mem_xs[(b, mt)], gmem, gmem_b, hT.t[1], mt * P, hT_b[1][mt])

    def mem_kv(b):
        memT = hT.t[1][:, :, 0:256]
        memT_b = hT_b[1][0:2]
        for c in range(2):
            bk, bkb = nb()

            def mm(q, c=c, bk=bk):
                r = None
                for k in range(8):
                    r = q.matmul(out=bk[:, 0:256], lhsT=wmem_bf[:, k, c * P:(c + 1) * P], rhs=memT[:, k, :],
                                 start=(k == 0), stop=(k == 7))
                return r
            op("pe", mm, reads=memT_b + wmem_b, excl=[bkb])
            op("act", lambda q, c=c, bk=bk: q.activation(out=mkT.t[b][:, c, :], in_=bk[:, 0:256], func=AF.Copy),
               writes=[mkT_b[b][c]], excl=[bkb])
        for mb in range(2):
            bk, bkb = nb()

            def mm(q, mb=mb, bk=bk):
                r = None
                for k in range(8):
                    r = q.matmul(out=bk[:, 0:256], lhsT=memT[:, k, mb * P:(mb + 1) * P], rhs=wmem_bf[:, k, 256:512],
                                 start=(k == 0), stop=(k == 7))
                return r
            op("pe", mm, reads=memT_b + wmem_b, excl=[bkb])
            op("dve", lambda q, mb=mb, bk=bk: q.tensor_copy(
                out=mvaug.t[b][:, mb, :, 0:64], in_=bk[:, 0:256].rearrange("p (h d) -> p h d", h=4)),
               writes=[mvaug_b[b][mb]], excl=[bkb])

    def feat_chunk(s, col, evac):
        sb = s % 2
        bk, bkb = nb()

        def mm(q):
            r = None
            for k in range(8):
                r = q.matmul(out=bk[:], lhsT=win_bf[:, k, col:col + P], rhs=hT.t[sb][:, k, :],
                             start=(k == 0), stop=(k == 7))
            return r
        op("pe", mm, reads=hT_b[sb] + win_bufs(col), excl=[bkb])
        evac(bk, bkb)

    ln_pending = [None]

    def ln_tail():
        if ln_pending[0] is None:
            return
        gi, vi = ln_pending[0]
        ln_pending[0] = None
        op("dve", lambda q: q.scalar_tensor_tensor(out=nm4.t[gi][:], in0=mv.t[gi][:, :, 0], scalar=-1.0,
                                                   in1=rs4.t[gi][:], op0=ALU.mult, op1=ALU.mult),
           reads=mv_g[gi] + [rs4.b[gi]], writes=[nm4.b[gi]])
        for g in range(4):
            op("dve", lambda q, g=g: q.tensor_scalar(
                out=vhat.t[vi][:, g * P:(g + 1) * P], in0=gv.t[gi][:, g * P:(g + 1) * P],
                scalar1=rs4.t[gi][:, g:g + 1], scalar2=nm4.t[gi][:, g:g + 1], op0=ALU.mult, op1=ALU.add),
               reads=[gv.b[gi], rs4.b[gi], nm4.b[gi]], writes=[vhat_g[vi][g]])

    def stageB_groups(s, part):
        sb = s % 2
        groups = []
        if part == 0:
            for c in range(2):
                groups.append(lambda c=c: feat_chunk(s, C_Q + c * P, lambda bk, bkb: op(
                    "dve", lambda q: q.tensor_copy(out=qT.t[sb][:, c, :], in_=bk[:]),
                    writes=[qT_b[sb][c]], excl=[bkb])))
            groups.append(lambda: feat_chunk(s, C_K, lambda bk, bkb: op(
                "dve", lambda q: q.tensor_copy(out=kT.t[s % 3][:], in_=bk[:]), writes=[kT.b[s % 3]], excl=[bkb])))
            for c in range(2):
                groups.append(lambda c=c: feat_chunk(s, C_MQ + c * P, lambda bk, bkb: op(
                    "dve", lambda q: q.tensor_copy(out=mqT.t[sb][:, c, :], in_=bk[:]),
                    writes=[mqT_b[sb][c]], excl=[bkb])))

            def g_sv():
                bk, bkb = nb()

                def mmsv(q):
                    r = None
                    for t in range(4):
                        for k in range(8):
                            r = q.matmul(out=bk[:, t * P:(t + 1) * P], lhsT=hT.t[sb][:, k, t * P:(t + 1) * P],
                                         rhs=win_bf[:, k, C_SV:C_SV + P], start=(k == 0), stop=(k == 7))
                    return r
                op("pe", mmsv, reads=hT_b[sb] + win_bufs(C_SV), excl=[bkb])
                op("dve", lambda q: q.tensor_copy(
                    out=Vaug.t[s % 3][:, :, :, 0:64], in_=bk[:].rearrange("p (t h d) -> p t h d", t=4, h=2)),
                   writes=[Vaug.b[s % 3]], excl=[bkb])
            groups.append(g_sv)
        elif part == 1:
            def g_v(t):
                n = s * 4 + t
                gi = n % 2
                vi = n % 8
                bk, bkb = nb()

                def mm(q):
                    r = None
                    for k in range(8):
                        r = q.matmul(out=bk[:], lhsT=hT.t[sb][:, k, t * P:(t + 1) * P], rhs=win_bf[:, k, C_V:C_V + 512],
                                     start=(k == 0), stop=(k == 7))
                    return r
                op("pe", mm, reads=[hT_b[sb][t]] + win_bufs(C_V), excl=[bkb])
                op("act", lambda q: q.activation(out=gv.t[gi][:], in_=bk[:], func=AF.Gelu_apprx_tanh),
                   writes=[gv.b[gi]], excl=[bkb])
                for g in range(4):
                    op("dve", lambda q, g=g: q.bn_stats(out=st.t[gi][:, g, :], in_=gv.t[gi][:, g * P:(g + 1) * P]),
                       reads=[gv.b[gi]], writes=[st_g[gi][g]])
                for g in range(4):
                    op("dve", lambda q, g=g: q.bn_aggr(out=mv.t[gi][:, g, :], in_=st.t[gi][:, g, :]),
                       reads=[st_g[gi][g]], writes=[mv_g[gi][g]])
                op("pool", lambda q: q.tensor_scalar(out=rs4.t[gi][:], in0=mv.t[gi][:, :, 1], scalar1=EPS,
                                                     scalar2=None, op0=ALU.add),
                   reads=mv_g[gi], writes=[rs4.b[gi]])
                op("pool", lambda q: q.tensor_tensor(out=rs4.t[gi][:], in0=rs4.t[gi][:], in1=mhalf[:], op=ALU.pow),
                   reads=[rs4.b[gi], mhalf_b], writes=[rs4.b[gi]])
                ln_tail()
                ln_pending[0] = (gi, vi)

            def g_u(c):
                feat_chunk(s, C_U + c * P, lambda bk, bkb: op(
                    "act", lambda q: q.activation(out=gu.t[sb][:, c, :], in_=bk[:], func=AF.Gelu_apprx_tanh),
                    writes=[gu_b[sb][c]], excl=[bkb]))
                ln_tail()
            for t in range(4):
                groups.append(lambda t=t: g_v(t))
            for c in range(4):
                groups.append(lambda c=c: g_u(c))
        elif part == 2:
            def g_z(c):
                def ev(bk, bkb):
                    zi = c % 2
                    op("act", lambda q: q.activation(out=sztmp.t[zi][:], in_=bk[:], func=AF.Silu),
                       writes=[sztmp.b[zi]], excl=[bkb])
                    op("dve", lambda q: q.tensor_tensor(out=gu.t[sb][:, c, :], in0=gu.t[sb][:, c, :], in1=sztmp.t[zi][:],
                                                        op=ALU.mult),
                       reads=[sztmp.b[zi], gu_b[sb][c]], writes=[gu_b[sb][c]])
                feat_chunk(s, C_Z + c * P, ev)
            for c in range(4):
                groups.append(lambda c=c: g_z(c))
        else:
            for c in range(4):
                groups.append(lambda c=c: feat_chunk(s, C_Z + (4 + c) * P, lambda bk, bkb: op(
                    "act", lambda q: q.activation(out=sz47.t[sb][:, c, :], in_=bk[:], func=AF.Silu),
                    writes=[sz47_b[sb][c]], excl=[bkb])))
        return groups

    def stageB_part(s, part):
        for g in stageB_groups(s, part):
            g()

    e_ctr = [0]
    swa_E = {}

    def swa_logits(n):
        s, t = divmod(n, 4)
        sb = s % 2
        has_prev = (n % 16) != 0

        def grp(kv):
            bk, bkb = nb()
            bv = bk[:].rearrange("p (j g c) -> p j g c", j=2, g=2)
            ps = slice(kv * 64, (kv + 1) * 64)
            rhs = qT.t[sb][ps, :, t * P:(t + 1) * P]
            if t > 0:
                kprev, kprev_b = kT.t[s % 3][ps, (t - 1) * P:t * P], kT.b[s % 3]
            else:
                kprev, kprev_b = kT.t[(s - 1) % 3][ps, 3 * P:4 * P], kT.b[(s - 1) % 3]

            def mm(q):
                if has_prev:
                    q.matmul(out=bv[:, 0, :, :], lhsT=kprev, rhs=rhs, start=True, stop=True)
                return q.matmul(out=bv[:, 1, :, :], lhsT=kT.t[s % 3][ps, t * P:(t + 1) * P], rhs=rhs,
                                start=True, stop=True)
            op("pe", mm, reads=qT_b[sb] + [kT.b[s % 3], kprev_b], excl=[bkb])
            c0 = 0 if has_prev else 256
            e0i = e_ctr[0] % 2
            e_ctr[0] += 1
            ei = (n % 4) * 2 + kv
            op("act", lambda q: q.activation(out=E0.t[e0i][:, c0:512], in_=bk[:, c0:512], func=AF.Exp, scale=0.125),
               writes=[E0.b[e0i]], excl=[bkb])
            op("dve", lambda q: q.tensor_tensor(
                out=E.t[ei][:, c0:512], in0=E0.t[e0i][:, c0:512], in1=EBT[:, kv * 512 + c0:(kv + 1) * 512], op=ALU.mult),
               reads=[E0.b[e0i], EBT_b], writes=[E.b[ei]])
            swa_E[(n, kv)] = ei
        return [lambda: grp(0), lambda: grp(1)]

    def mem_logits(s):
        sb = s % 2
        b = s // 4

        def grp(c, mb):
            bks = [nb(), nb()]
            for hh in range(2):
                h = 2 * c + hh
                ps = slice(hh * 64, hh * 64 + 64)
                bk, bkb = bks[hh]
                op("pe", lambda q, bk=bk, ps=ps: q.matmul(
                    out=bk[:], lhsT=mkT.t[b][ps, c, mb * P:(mb + 1) * P], rhs=mqT.t[sb][ps, c, :],
                    start=True, stop=True),
                   reads=[mkT_b[b][c], mqT_b[sb][c]], excl=[bkb])
            for hh in range(2):
                h = 2 * c + hh
                bk, bkb = bks[hh]
                op("act", lambda q, bk=bk, h=h: q.activation(out=Em[:, h, mb, :], in_=bk[:], func=AF.Exp, scale=0.125),
                   writes=[Em_b[h][mb]], excl=[bkb])
        return [lambda c=c, mb=mb: grp(c, mb) for c in range(2) for mb in range(2)]

    def stageC_rest(n):
        s, t = divmod(n, 4)
        sb = s % 2
        b = s // 4
        has_prev = (n % 16) != 0
        yi = n % 3
        ci = n % 2
        vi = n % 8

        def g_sp():
            bk, bkb = nb()

            def mmsp(q):
                r = None
                for g in range(4):
                    r = q.matmul(out=bk[:, g * P:(g + 1) * P], lhsT=vhat.t[vi][:, g * P:(g + 1) * P],
                                 rhs=wsT_bf[:, g * P:(g + 1) * P], start=True, stop=True)
                return r
            op("pe", mmsp, reads=vhat_g[vi] + [wsT_bf_b], excl=[bkb])
            for g in range(4):
                op("dve", lambda q, g=g: q.scalar_tensor_tensor(
                    out=tA.t[ci][:, g * P:(g + 1) * P], in0=bk[:, g * P:(g + 1) * P], scalar=vgcol[:, g:g + 1],
                    in1=biasA[:, g * P:(g + 1) * P], op0=ALU.mult, op1=ALU.add),
                   reads=[vgcol_b, biasA_b], writes=[tA_g[ci][g]], excl=[bkb])
            op("pool", lambda q: q.tensor_tensor(
                out=yT.t[yi][:, 0:4, :], in0=tA.t[ci][:].rearrange("p (g c) -> p g c", g=4),
                in1=gu.t[sb][:, :, t * P:(t + 1) * P], op=ALU.mult),
               reads=tA_g[ci] + gu_b[sb], writes=[yT_a[yi]])

        def g_pvs():
            bk, bkb = nb()
            pv = bk[:, 0:4 * VW].rearrange("p (h w) -> p h w", h=4)
            if t > 0:
                vprev, vprev_b = Vaug.t[s % 3][:, t - 1, :, :], Vaug.b[s % 3]
            else:
                vprev, vprev_b = Vaug.t[(s - 1) % 3][:, 3, :, :], Vaug.b[(s - 1) % 3]
            vcur = Vaug.t[s % 3][:, t, :, :]
            eis = [swa_E[(n, 0)], swa_E[(n, 1)]]

            def mmpv(q):
                r = None
                for kv in range(2):
                    ev = E.t[eis[kv]][:].rearrange("p (j g c) -> p j g c", j=2, g=2)
                    for g in range(2):
                        h = 2 * kv + g
                        if has_prev:
                            q.matmul(out=pv[:, h, 0:65], lhsT=ev[:, 0, g, :], rhs=vprev[:, kv, 0:65], start=True, stop=False)
                        r = q.matmul(out=pv[:, h, 0:65], lhsT=ev[:, 1, g, :], rhs=vcur[:, kv, 0:65],
                                     start=(not has_prev), stop=True)
                return r
            op("pe", mmpv, reads=[E.b[eis[0]], E.b[eis[1]], Vaug.b[s % 3], vprev_b], excl=[bkb])
            op("dve", lambda q: q.tensor_tensor(out=den.t[ci][:], in0=pv[:, :, 64], in1=esink[:], op=ALU.add),
               reads=[esink_b], writes=[den.b[ci]], excl=[bkb])
            op("dve", lambda q: q.reciprocal(out=rden.t[ci][:], in_=den.t[ci][:]), reads=[den.b[ci]], writes=[rden.b[ci]])
            op("dve", lambda q: q.tensor_tensor(
                out=ybc.t[ci][:, 0:256].rearrange("p (h d) -> p h d", h=4), in0=pv[:, :, 0:64],
                in1=rden.t[ci][:].unsqueeze(2).to_broadcast([P, 4, 64]), op=ALU.mult),
               reads=[rden.b[ci]], writes=[ybc_s[ci]], excl=[bkb])

        def g_pvm():
            bk, bkb = nb()
            pm = bk[:, 0:4 * VW].rearrange("p (h w) -> p h w", h=4)

            def mmpm(q):
                r = None
                for h in range(4):
                    for mb in range(2):
                        r = q.matmul(out=pm[:, h, 0:65], lhsT=Em[:, h, mb, t * P:(t + 1) * P],
                                     rhs=mvaug.t[b][:, mb, h, 0:65], start=(mb == 0), stop=(mb == 1))
                return r
            op("pe", mmpm, reads=[x for row in Em_b for x in row] + mvaug_b[b], excl=[bkb])
            op("dve", lambda q: q.reciprocal(out=rdm.t[ci][:], in_=pm[:, :, 64]), writes=[rdm.b[ci]], excl=[bkb])
            op("dve", lambda q: q.tensor_tensor(
                out=ybc.t[ci][:, 256:512].rearrange("p (h d) -> p h d", h=4), in0=pm[:, :, 0:64],
                in1=rdm.t[ci][:].unsqueeze(2).to_broadcast([P, 4, 64]), op=ALU.mult),
               reads=[rdm.b[ci]], writes=[ybc_m[ci]], excl=[bkb])
        return [g_sp, g_pvs, g_pvm]

    def stageC_tr(n):
        s, t = divmod(n, 4)
        sb = s % 2
        yi = n % 3
        ci = n % 2

        def grp():
            bk, bkb = nb()
            bkv = bk[:, 0:256].bitcast(BF16).rearrange("p (k c) -> p k c", k=4)

            def tr(q):
                r = None
                for c in range(4):
                    r = q.transpose(out=bkv[:, c, :], in_=ybc.t[ci][:, c * P:(c + 1) * P], identity=ident_bf[:])
                return r
            op("pe", tr, reads=[ybc_s[ci], ybc_m[ci], ident_bf_b], excl=[bkb])
            op("dve", lambda q: q.tensor_tensor(out=yT.t[yi][:, 4:8, :], in0=bkv, in1=sz47.t[sb][:, :, t * P:(t + 1) * P],
                                                op=ALU.mult),
               reads=sz47_b[sb], writes=[yT_b[yi]], excl=[bkb])
        return [grp]

    def D_load(n):
        di = n % 2
        ft, fb = FB.t[XD[di]], FB.b[XD[di]]
        op("sp", lambda q: q.dma_start(out=ft[:], in_=x_d[n * P:(n + 1) * P, :]), writes=[fb], dma_key=f"F{XD[di]}")

    def stageD(n):
        yi = n % 3
        di = n % 2
        tOt = FB.t[TO]

        def grp(half):
            bk, bkb = nb()
            hs = slice(half * 512, (half + 1) * 512)

            def mm(q):
                r = None
                for k in range(8):
                    r = q.matmul(out=bk[:], lhsT=yT.t[yi][:, k, :], rhs=wout_bf[:, k, hs], start=(k == 0), stop=(k == 7))
                return r
            op("pe", mm, reads=[yT_a[yi], yT_b[yi], wout_b[half]], excl=[bkb])
            op("act", lambda q: q.activation(out=junk[:], in_=bk[:], func=AF.Square,
                                             accum_out=ss2.t[di][:, half:half + 1]),
               writes=[ss2_b[di][half], junk_b], excl=[bkb])
            op("dve", lambda q: q.tensor_tensor(out=tOt[:, hs], in0=bk[:], in1=gpost[:, hs], op=ALU.mult),
               reads=[gpost_b], writes=[tO_b[half]], excl=[bkb])
            if half == 1:
                op("pool", lambda q: q.tensor_scalar(out=r2.t[di][:], in0=ss2.t[di][:, 0:1], scalar1=ss2.t[di][:, 1:2],
                                                     scalar2=D * EPS, op0=ALU.add, op1=ALU.add),
                   reads=ss2_b[di], writes=[r2.b[di]])
                op("pool", lambda q: q.tensor_tensor(out=r2.t[di][:], in0=r2.t[di][:], in1=mhalf[:, 0:1], op=ALU.pow),
                   reads=[r2.b[di], mhalf_b], writes=[r2.b[di]])
                op("pool", lambda q: q.tensor_scalar(out=r2.t[di][:], in0=r2.t[di][:], scalar1=32.0, scalar2=None,
                                                     op0=ALU.mult),
                   reads=[r2.b[di]], writes=[r2.b[di]])
        return [lambda: grp(0), lambda: grp(1)]

    def D_fin(n):
        di = n % 2
        ft, fb = FB.t[XD[di]], FB.b[XD[di]]
        tOt = FB.t[TO]
        op("dve", lambda q: q.scalar_tensor_tensor(out=ft[:], in0=tOt[:], scalar=r2.t[di][:, 0:1], in1=ft[:],
                                                   op0=ALU.mult, op1=ALU.add),
           reads=tO_b + [r2.b[di], fb], writes=[fb])
        op("sp", lambda q: q.dma_start(out=out_d[n * P:(n + 1) * P, :], in_=ft[:]), reads=[fb],
           dma_key=f"O{di}", is_store=True)

    a_slot = {}
    a_xs = {}

    def A_ld(n):
        if 0 <= n < NT:
            a_slot[n] = A_load(x_d[n * P:(n + 1) * P, :])

    def A_pr(n):
        if 0 <= n < NT:
            a_xs[n] = A_pre(a_slot[n])

    def A_pr2(n):
        if 0 <= n < NT:
            A_pre2(a_slot[n], a_xs[n])

    def A_t(n):
        if 0 <= n < NT:
            s_, t_ = divmod(n, 4)
            A_tr(a_xs[n], gpre, gpre_b, hT.t[s_ % 2], t_ * P, hT_b[s_ % 2][t_])

    A_ld(0)
    A_ld(1)
    A_ld(2)
    a_slot[3] = A_load(x_d[3 * P:4 * P, :], force=3)
    load_win_piece(0)
    load_win_piece(1)
    load_win_piece(3)
    load_win_piece(2)
    load_wmem()
    setup_first()
    A_pr(0)
    A_pr2(0)
    A_pr(1)
    A_pr2(1)
    A_t(0)
    A_pr(2)
    A_pr2(2)
    A_t(1)
    A_pr(3)
    A_pr2(3)
    A_t(2)
    A_t(3)
    load_win_piece(4)
    load_win_piece(5)
    load_wout()
    mem_pre(0)
    setup_rest()
    stageB_part(0, 0)
    stageB_part(0, 1)
    mem_tr(0)
    mem_pre(1)
    mem_kv(0)
    setup_compute()
    load_const(gpost[:], gpost_d, gpost_b)
    A_ld(4)
    mem_tr(1)
    A_ld(5)
    A_pr(4)
    A_pr2(4)
    A_ld(6)
    A_pr(5)
    A_pr2(5)
    mem_kv(1)
    setup_ebt()
    A_t(4)
    A_ld(7)
    A_pr(6)
    A_pr2(6)
    A_t(5)
    stageB_part(0, 2)
    A_ld(8)
    A_pr(7)
    A_pr2(7)
    A_t(6)
    stageB_part(0, 3)
    A_t(7)
    def interleave(a, b):
        out = []
        for i in range(max(len(a), len(b))):
            if i < len(a):
                out.append(a[i])
            if i < len(b):
                out.append(b[i])
        return out

    for q in range(-2, NT + 2):
        A_ld(q + 11)
        if 0 <= q - 1 < NT:
            D_load(q - 1)
        A_pr(q + 10)
        cg = stageC_rest(q + 1) if 0 <= q + 1 < NT else []
        sg = []
        mg = []
        if q % 2 == 0 and 0 <= q + 2 < NT:
            sg = swa_logits(q + 2) + swa_logits(q + 3)
            if (q + 2) % 4 == 0:
                mg = mem_logits((q + 2) // 4)
        bg = stageB_groups((q + 6) // 4, (q + 6) % 4) if 0 <= q + 6 < NT else []
        dg = stageD(q - 1) if 0 <= q - 1 < NT else []
        tg = stageC_tr(q) if 0 <= q < NT else []
        ag = [lambda q=q: A_t(q + 10)] if 0 <= q + 10 < NT else []
        part = (q + 6) % 4
        if mg:
            short = interleave(cg, sg) + mg
            long_ = dg + bg + tg + ag
        elif sg:
            short = interleave(sg, cg)
            long_ = dg + tg + ag + bg
        else:
            short = cg
            long_ = bg[:len(bg) // 2] + dg + bg[len(bg) // 2:] + tg + ag
        if mg or sg:
            seq = []
            si = 0
            for lg in long_:
                seq += short[si:si + 2]
                si += 2
                seq.append(lg)
            seq += short[si:]
        else:
            seq = interleave(short, long_)
        if 0 <= q - 2 < NT:
            D_fin(q - 2)
        for i, g in enumerate(seq):
            g()
            if i == 0:
                A_pr2(q + 10)
        if not seq:
            A_pr2(q + 10)
        if 0 <= q + 6 < NT and part == 1 and q + 6 + 4 >= NT:
            ln_tail()
    pg.finish()
    pg.emit(nc)
    return nc


def _t5_buckets(dist):
    n = np.maximum(dist, 0)
    max_exact = 16
    large = max_exact + (np.log(np.maximum(n, 1) / max_exact) / np.log(128 / max_exact) * (32 - max_exact)).astype(np.int32)
    large = np.minimum(large, 31)
    return np.where(n < max_exact, n, large).astype(np.int32)


def _prep_shared(pre_norm_g, post_norm_g, mem_norm_g, w_in, w_mem_kv, v_norm_g, v_norm_b, w_spatial, b_spatial,
                 attn_sinks, rel_bias, w_out):
    f = lambda a: np.ascontiguousarray(np.asarray(a, dtype=np.float32))
    w = np.asarray(w_in, dtype=np.float32)[0]
    sq = 1024
    cols = np.concatenate([
        np.arange(0, 512),
        sq + np.concatenate([np.arange(0, 64), np.arange(128, 192), np.arange(64, 128), np.arange(192, 256)]),
        np.arange(1280, 1408),
        np.arange(1536, 1792),
        np.arange(1792, 2816),
        np.arange(512, 1024),
        np.arange(1408, 1536),
    ])
    d = {}
    d["win"] = f(w[:, cols])
    d["wout"] = f(np.asarray(w_out)[0])
    d["wmem"] = f(np.asarray(w_mem_kv)[0])
    d["gpre"] = f(np.asarray(pre_norm_g)[0].reshape(8, P).T)
    d["gmem"] = f(np.asarray(mem_norm_g)[0].reshape(8, P).T)
    d["gpost"] = f(np.broadcast_to(np.asarray(post_norm_g)[0][None, :], (P, D)))
    d["vgcol"] = f(np.asarray(v_norm_g)[0].reshape(4, P).T)
    d["vbb"] = f(np.broadcast_to(np.asarray(v_norm_b)[0][None, :], (P, 512)))
    d["wsT"] = f(np.asarray(w_spatial)[0].transpose(2, 0, 1).reshape(P, 512))
    sidx = np.arange(P)[:, None]
    tidx = np.arange(P)[None, :]
    cm = (tidx >= sidx).astype(np.float32)
    d["cmask"] = f(np.broadcast_to(cm[:, None, :], (P, 4, P)).reshape(P, 512))
    d["bsrow"] = f(np.asarray(b_spatial)[0].reshape(1, 512))
    jp = np.arange(P)[:, None, None]
    jb = np.arange(2)[None, :, None]
    tq = np.arange(P)[None, None, :]
    dist = tq + P - (jb * P + jp)
    valid = (dist >= 0) & (dist < 128)
    bk = _t5_buckets(dist)
    rb = np.asarray(rel_bias, dtype=np.float32)
    eb = np.zeros((P, 2, 2, 2, P), np.float32)
    for kv in range(2):
        for g in range(2):
            eb[:, kv, :, g, :] = rb[bk, 2 * kv + g]
    d["ebias"] = f(eb.reshape(P, 1024))
    em = np.broadcast_to(valid[:, None, :, None, :], (P, 2, 2, 2, P)).astype(np.float32)
    d["emask"] = f(em.reshape(P, 1024))
    d["sinks"] = f(np.broadcast_to(np.asarray(attn_sinks)[0][None, :], (P, 4)))
    d["ident"] = np.eye(P, dtype=np.float32)
    return d


_NC_CACHE = {}


def kernel(x, mem, pre_norm_g, post_norm_g, mem_norm_g, w_in, w_mem_kv, v_norm_g, v_norm_b,
           w_spatial, b_spatial, attn_sinks, rel_bias, w_out):
    x = np.asarray(x, dtype=np.float32)
    mem = np.asarray(mem, dtype=np.float32)
    shared = _prep_shared(pre_norm_g, post_norm_g, mem_norm_g, w_in, w_mem_kv, v_norm_g, v_norm_b, w_spatial,
                          b_spatial, attn_sinks, rel_bias, w_out)
    n = 8
    in_maps = []
    for c in range(n):
        m = dict(shared)
        m["x"] = np.ascontiguousarray(x[2 * c:2 * c + 2].reshape(NT * P, D))
        m["mem"] = np.ascontiguousarray(mem[2 * c:2 * c + 2].reshape(512, D))
        in_maps.append(m)
    if "nc" not in _NC_CACHE:
        _NC_CACHE["nc"] = build_program()
    nc = _NC_CACHE["nc"]
    res = run_bass_kernel_spmd(nc, in_maps, core_ids=list(range(n)))
    out = np.stack([np.asarray(r["out"], dtype=np.float32).reshape(2, 2048, D) for r in res.results], axis=0)
    return out.reshape(16, 2048, D)
```
